# Optimizing a Trainium2 kernel written in Bass

```python
import jax, jax.numpy as jnp
from jax import lax
import numpy as np

D_MODEL = 1024
BATCH = 32
SEQ = 256
DEPTH = 4
DEC_BATCH = 4
DEC_SEQ = 1024
PAST_LEN = 256

GRID_W = 64
D_LRU = D_MODEL // 2
LRU_HEADS = 8
LRU_HEAD_DIM = D_LRU // LRU_HEADS
CONV_WIDTH = 4
CONV_PAD_LEFT = 2
LRU_C = 8.0
D_SGU = D_MODEL // 2
SGU_GROUPS = 4
SGU_GROUP_DIM = D_SGU // SGU_GROUPS
CHUNK = 2 * GRID_W
FNET_GROUPS = 4
FNET_GROUP_DIM = D_MODEL // FNET_GROUPS
D_FF = 4 * D_MODEL
N_AB_LAYERS = (DEPTH + 1) // 2
N_C_LAYERS = DEPTH // 2
N_MOD = 6
DEEPNORM_ALPHA = (2.0 * DEPTH) ** 0.25
DEEPNORM_BETA = (8.0 * DEPTH) ** -0.25
LN_EPS = 1e-5

kernel_name = "hybrid_lru_sgu_fnet_diffusion_step"


def layer_norm(x, g, b):
    xf = x.astype(jnp.float32)
    mu = jnp.mean(xf, axis=-1, keepdims=True)
    var = jnp.mean(jnp.square(xf - mu), axis=-1, keepdims=True)
    return ((xf - mu) * lax.rsqrt(var + LN_EPS) * g + b).astype(x.dtype)


def ada_mod(cond, w, b):
    m = jax.nn.silu(cond) @ w + b
    return jnp.split(m[:, None, :], N_MOD, axis=-1)


def centred_dwconv(x, w, b):
    S = x.shape[1]
    xp = jnp.pad(x, ((0, 0), (CONV_PAD_LEFT, CONV_WIDTH - 1 - CONV_PAD_LEFT), (0, 0)))
    return sum(xp[:, k:k + S] * w[k] for k in range(CONV_WIDTH)) + b


def _affine_combine(left, right):
    a_l, b_l = left
    a_r, b_r = right
    return a_l * a_r, a_r * b_l + b_r


def rg_lru(xc, wa, ba, wx, bx, lam, h0):
    B_, S = xc.shape[:2]
    xh = xc.reshape(B_, S, LRU_HEADS, LRU_HEAD_DIM)
    r = jax.nn.sigmoid(jnp.einsum('bshi,hij->bshj', xh, wa.astype(jnp.float32)).reshape(B_, S, D_LRU) + ba)
    i = jax.nn.sigmoid(jnp.einsum('bshi,hij->bshj', xh, wx.astype(jnp.float32)).reshape(B_, S, D_LRU) + bx)
    log_a = -LRU_C * r * jax.nn.softplus(-lam.astype(jnp.float32))
    a = jnp.exp(log_a)
    u = jnp.sqrt(-jnp.expm1(2.0 * log_a)) * (i * xc)
    u = u.at[:, 0].add(a[:, 0] * h0.astype(jnp.float32))
    _, h = lax.associative_scan(_affine_combine, (a, u), axis=1)
    return h


def lru_mixer(x_br, gate_br, j, h0_f, h0_b, conv_w, conv_b, lru_wa, lru_ba, lru_wx, lru_bx, lru_lam):
    xc = centred_dwconv(x_br, conv_w[j], conv_b[j]).astype(jnp.float32)
    h_f = rg_lru(xc, lru_wa[j, 0], lru_ba[j, 0], lru_wx[j, 0], lru_bx[j, 0], lru_lam[j, 0], h0_f)
    h_b = jnp.flip(rg_lru(jnp.flip(xc, 1), lru_wa[j, 1], lru_ba[j, 1], lru_wx[j, 1], lru_bx[j, 1],
                          lru_lam[j, 1], h0_b), 1)
    y = ((h_f + h_b) * jax.nn.gelu(gate_br.astype(jnp.float32))).astype(x_br.dtype)
    return y, h_f, h_b


def sgu_mixer(u, v, g, b, ws, bs):
    B_, S = u.shape[:2]
    u = jax.nn.gelu(u)
    v = layer_norm(jax.nn.gelu(v), g, b)
    vc = v.reshape(B_, S // CHUNK, CHUNK, SGU_GROUPS, SGU_GROUP_DIM)
    mix = jnp.einsum('gpq,bcqgd->bcpgd', ws, vc) + bs.T[None, None, :, :, None]
    return u * mix.reshape(B_, S, D_SGU)


def fourier_mixer(h):
    B_, S = h.shape[:2]
    hg = h.astype(jnp.float32).reshape(B_, S, FNET_GROUPS, FNET_GROUP_DIM)
    f = jnp.fft.fft2(hg, axes=(1, 3), norm='ortho').real
    return f.reshape(B_, S, D_MODEL).astype(h.dtype)


def run_trunk(x, cond, lru_h0, collect_state, w_ada, b_ada, w_in_ab, conv_w, conv_b, lru_wa, lru_ba,
              lru_wx, lru_bx, lru_lam, sgu_ln_g, sgu_ln_b, sgu_ws, sgu_bs, w_out_ab, w_out_c,
              ffn_w1, ffn_w2, ln_g, ln_b):
    finals = []
    for l in range(DEPTH):
        sh1, sc1, g1, sh2, sc2, g2 = ada_mod(cond, w_ada[l], b_ada[l])
        h = x * (1 + sc1) + sh1
        j = l // 2
        if l % 2 == 0:
            proj = h @ w_in_ab[j]
            xa, ga, ub, vb = jnp.split(proj, 4, axis=-1)
            ya, h_f, h_b = lru_mixer(xa, ga, j, lru_h0[:, j, 0], lru_h0[:, j, 1], conv_w, conv_b,
                                     lru_wa, lru_ba, lru_wx, lru_bx, lru_lam)
            yb = sgu_mixer(ub, vb, sgu_ln_g[j], sgu_ln_b[j], sgu_ws[j], sgu_bs[j])
            mix = jnp.concatenate([ya, yb], axis=-1) @ w_out_ab[j]
            if collect_state:
                finals.append(jnp.stack([h_f[:, -1], h_b[:, 0]], axis=1).astype(x.dtype))
        else:
            mix = fourier_mixer(h) @ w_out_c[j]
        x = layer_norm(DEEPNORM_ALPHA * x + g1 * mix, ln_g[l, 0], ln_b[l, 0])
        h = x * (1 + sc2) + sh2
        f = jnp.square(jax.nn.relu(h @ ffn_w1[l])) @ ffn_w2[l]
        x = layer_norm(DEEPNORM_ALPHA * x + g2 * f, ln_g[l, 1], ln_b[l, 1])
    state = jnp.stack(finals, axis=1) if collect_state else None
    return x, state


def setup_inputs(seed: int = 0) -> dict:
    key = jax.random.key(seed)
    ks = jax.random.split(key, 32)
    nrm = jax.random.normal
    D = D_MODEL
    a0 = jax.random.uniform(ks[14], (N_AB_LAYERS, 2, D_LRU), jnp.float32, minval=0.9, maxval=0.999)
    return {
        "x_prompt": nrm(ks[0], (BATCH, SEQ, D), jnp.float32),
        "x_sample": nrm(ks[1], (DEC_BATCH, DEC_SEQ, D), jnp.float32),
        "state_lru": 0.5 * nrm(ks[2], (DEC_BATCH, N_AB_LAYERS, 2, D_LRU), jnp.float32),
        "c": nrm(ks[3], (DEC_BATCH, D), jnp.float32),
        "c_ctx": nrm(ks[4], (D,), jnp.float32),
        "w_ada": nrm(ks[5], (DEPTH, D, N_MOD * D), jnp.float32) * D ** -0.5,
        "b_ada": 0.02 * nrm(ks[6], (DEPTH, N_MOD * D), jnp.float32),
        "w_in_ab": nrm(ks[7], (N_AB_LAYERS, D, 2 * D_LRU + 2 * D_SGU), jnp.float32) * D ** -0.5,
        "conv_w": nrm(ks[8], (N_AB_LAYERS, CONV_WIDTH, D_LRU), jnp.float32) * CONV_WIDTH ** -0.5,
        "conv_b": 0.02 * nrm(ks[9], (N_AB_LAYERS, D_LRU), jnp.float32),
        "lru_wa": nrm(ks[10], (N_AB_LAYERS, 2, LRU_HEADS, LRU_HEAD_DIM, LRU_HEAD_DIM), jnp.float32) * LRU_HEAD_DIM ** -0.5,
        "lru_ba": 0.02 * nrm(ks[11], (N_AB_LAYERS, 2, D_LRU), jnp.float32),
        "lru_wx": nrm(ks[12], (N_AB_LAYERS, 2, LRU_HEADS, LRU_HEAD_DIM, LRU_HEAD_DIM), jnp.float32) * LRU_HEAD_DIM ** -0.5,
        "lru_bx": 0.02 * nrm(ks[13], (N_AB_LAYERS, 2, D_LRU), jnp.float32),
        "lru_lam": jnp.log(a0) - jnp.log1p(-a0),
        "sgu_ln_g": 1.0 + 0.02 * nrm(ks[15], (N_AB_LAYERS, D_SGU), jnp.float32),
        "sgu_ln_b": 0.02 * nrm(ks[16], (N_AB_LAYERS, D_SGU), jnp.float32),
        "sgu_ws": nrm(ks[17], (N_AB_LAYERS, SGU_GROUPS, CHUNK, CHUNK), jnp.float32) * CHUNK ** -0.5,
        "sgu_bs": 1.0 + 0.02 * nrm(ks[18], (N_AB_LAYERS, SGU_GROUPS, CHUNK), jnp.float32),
        "w_out_ab": nrm(ks[19], (N_AB_LAYERS, D_LRU + D_SGU, D), jnp.float32) * (D_LRU + D_SGU) ** -0.5 * DEEPNORM_BETA,
        "w_out_c": nrm(ks[20], (N_C_LAYERS, D, D), jnp.float32) * D ** -0.5 * DEEPNORM_BETA,
        "ffn_w1": nrm(ks[21], (DEPTH, D, D_FF), jnp.float32) * D ** -0.5,
        "ffn_w2": nrm(ks[22], (DEPTH, D_FF, D), jnp.float32) * D_FF ** -0.5 * DEEPNORM_BETA,
        "ln_g": 1.0 + 0.02 * nrm(ks[23], (DEPTH, 2, D), jnp.float32),
        "ln_b": 0.02 * nrm(ks[24], (DEPTH, 2, D), jnp.float32),
    }


def reference(x_prompt, x_sample, state_lru, c, c_ctx, w_ada, b_ada, w_in_ab, conv_w, conv_b,
              lru_wa, lru_ba, lru_wx, lru_bx, lru_lam, sgu_ln_g, sgu_ln_b, sgu_ws, sgu_bs,
              w_out_ab, w_out_c, ffn_w1, ffn_w2, ln_g, ln_b):
    weights = (w_ada, b_ada, w_in_ab, conv_w, conv_b, lru_wa, lru_ba, lru_wx, lru_bx, lru_lam,
               sgu_ln_g, sgu_ln_b, sgu_ws, sgu_bs, w_out_ab, w_out_c, ffn_w1, ffn_w2, ln_g, ln_b)
    zero_h0 = jnp.zeros((x_prompt.shape[0], N_AB_LAYERS, 2, D_LRU), jnp.float32)
    y_prompt, new_state_lru = run_trunk(x_prompt, c_ctx[None, :], zero_h0, True, *weights)
    y_sample, _ = run_trunk(x_sample, c, state_lru, False, *weights)
    return (y_prompt, y_sample, new_state_lru)
```

```python
import itertools
import numpy as np
from contextlib import ExitStack
import ml_dtypes
import concourse.bass as bass
import concourse.mybir as mybir
from concourse.bass_utils import run_bass_kernel_spmd

F32 = mybir.dt.float32
BF16 = mybir.dt.bfloat16
AF = mybir.ActivationFunctionType
ALU = mybir.AluOpType

D = 1024
T = 1536
KC = 8
NB = 3
NT = 12
NSEG = 6
DFF = 4096
FC = 32
DEPTH = 4
ALPHA = (2.0 * DEPTH) ** 0.25
EPS = 1e-5
NSLOT = 4

R_BADA = 0
R_LNG = 192
R_LNB = 256
R_CONVW = 320
R_CONVB = 352
R_BA = 360
R_BX = 376
R_LAM = 392
R_COND = 408
R_H0 = 424
R_SGUG = 440
NROWS = 512


class Buf:
    __slots__ = ("name", "w", "r", "rd")

    def __init__(self, name):
        self.name = name
        self.w = None
        self.r = {}
        self.rd = []


class Op:
    __slots__ = ("eng", "fn", "deps", "flag", "cnt", "dma", "sem", "semval")

    def __init__(self, eng, fn, dma):
        self.eng = eng
        self.fn = fn
        self.dma = dma
        self.deps = []
        self.flag = False
        self.cnt = 0
        self.sem = None
        self.semval = 0


class Prog:
    def __init__(self, n_dma_sems=40, n_sw=24):
        self.q = {"pe": [], "act": [], "dve": [], "pool": [], "sp": []}
        self.n_dma_sems = n_dma_sems
        self.dma_last = [None] * n_dma_sems
        self.dma_uses = [0] * n_dma_sems
        self.rng = {"pool": (0, n_sw), "sp": (n_sw, n_dma_sems)}
        self.dma_rr = {"pool": 0, "sp": n_sw}
        self.out_dmas = []

    def add(self, eng, fn, reads=(), writes=(), dma=False, is_out=False):
        op = Op(eng, fn, dma)
        deps = {}
        for b in reads:
            if b.w is not None:
                deps[id(b.w)] = b.w
        for b in writes:
            if b.w is not None:
                deps[id(b.w)] = b.w
            for r in b.r.values():
                deps[id(r)] = r
            for r in b.rd:
                deps[id(r)] = r
        if dma:
            k = self.dma_rr[eng]
            lo, hi = self.rng[eng]
            self.dma_rr[eng] = lo + (k + 1 - lo) % (hi - lo)
            prev = self.dma_last[k]
            if prev is not None:
                deps[id(prev)] = prev
            self.dma_uses[k] += 1
            op.sem = k
            op.semval = 16 * self.dma_uses[k]
            self.dma_last[k] = op
        dl = []
        for d in deps.values():
            if d is op:
                continue
            if eng == "pe" and (not dma) and d.eng == "pe" and not d.dma:
                continue
            dl.append(d)
            d.flag = True
        op.deps = dl
        for b in reads:
            if dma:
                b.rd.append(op)
            else:
                b.r[eng] = op
        for b in writes:
            b.w = op
            b.r = {}
            b.rd = []
        self.q[eng].append(op)
        if is_out:
            self.out_dmas.append(op)
        return op

    def finish(self):
        op = Op("sp", None, False)
        op.deps = list(self.out_dmas)
        for d in op.deps:
            d.flag = True
        self.q["sp"].append(op)
        for e, ops in self.q.items():
            c = 0
            for o in ops:
                if o.dma or o.fn is None:
                    continue
                if o.flag:
                    c += 1
                    o.cnt = c

    def emit(self, nc, es):
        esem = {e: es.enter_context(nc.semaphore("s_" + e)) for e in self.q}
        dsem = [es.enter_context(nc.semaphore("d_%d" % i)) for i in range(self.n_dma_sems)]
        block = es.enter_context(nc.Block())
        stats = {}

        def run(engname, e):
            waited = {}
            nw = 0
            for op in self.q[engname]:
                need = {}
                for d in op.deps:
                    if d.dma:
                        s, v, key = dsem[d.sem], d.semval, ("d", d.sem)
                    else:
                        s, v, key = esem[d.eng], d.cnt, ("e", d.eng)
                    if waited.get(key, 0) >= v:
                        continue
                    if key not in need or need[key][1] < v:
                        need[key] = (s, v)
                items = list(need.items())
                attach = None
                if op.fn is not None and items:
                    attach = items.pop()
                for key, (s, v) in items:
                    e.wait_ge(s, v)
                    waited[key] = v
                    nw += 1
                if op.fn is None:
                    continue
                ins = op.fn(e)
                if attach is not None:
                    key, (s, v) = attach
                    ins._wait_ge(s, v)
                    waited[key] = v
                if op.dma:
                    ins.then_inc(dsem[op.sem], 16)
                elif op.flag:
                    ins.then_inc(esem[engname], 1)
            stats[engname] = (len(self.q[engname]), nw)

        @block.tensor
        def _(e):
            run("pe", e)

        @block.scalar
        def _(e):
            run("act", e)

        @block.vector
        def _(e):
            run("dve", e)

        @block.gpsimd
        def _(e):
            run("pool", e)

        @block.sync
        def _(e):
            run("sp", e)

        return stats


def build_program():
    nc = bass.Bass("TRN2", target_bir_lowering=False)
    P = Prog()
    es = ExitStack()

    def din(name, shape, dt=F32):
        return nc.dram_tensor(name, list(shape), dt, kind="ExternalInput").ap()

    def dout(name, shape, dt=F32):
        return nc.dram_tensor(name, list(shape), dt, kind="ExternalOutput").ap()

    xin = din("xin", [T, D])
    params = din("params", [NROWS, 128])
    cfd = din("cf", [128, 1])
    pd1 = din("pd1", [2, 1024, 1024], BF16)
    pd2 = din("pd2", [2, 512, 512], BF16)
    cs256 = din("cs256", [256, 512], BF16)
    w_ada = din("w_ada", [4, D, 6 * D])
    w_in_ab = din("w_in_ab", [2, D, 2048])
    lru_wa = din("lru_wa", [2, 2, 8, 64, 64])
    lru_wx = din("lru_wx", [2, 2, 8, 64, 64])
    sgu_ln_g = din("sgu_ln_g", [2, 512])
    sgu_ln_b = din("sgu_ln_b", [2, 512])
    sgu_ws = din("sgu_ws", [2, 4, 128, 128])
    sgu_bs = din("sgu_bs", [2, 4, 128])
    w_out_ab = din("w_out_ab", [2, D, D])
    w_out_c = din("w_out_c", [2, D, D])
    ffn_w1 = din("ffn_w1", [4, D, DFF])
    ffn_w2 = din("ffn_w2", [4, DFF, D])
    yout = dout("yout", [T, D])
    sout = dout("sout", [96, 128])

    def sb(name, shape, dt=F32):
        return es.enter_context(nc.sbuf_tensor(name, list(shape), dt))

    XS = sb("XS", [128, KC, T])
    HB = sb("HB", [128, KC, T], BF16)
    BIG = sb("BIG", [128, 24576])
    RING = sb("RING", [128, NSLOT, 4096], BF16)
    PT = sb("PT", [128, 448])
    MOD = sb("MOD", [128, 2, 48, 2])
    MA = sb("MA", [128, 512])
    SCB = sb("SCB", [128, 16], BF16)
    IDENT = sb("IDENT", [128, 128])
    ONES = sb("ONES", [128, 128])
    ONESB = sb("ONESB", [128, 128], BF16)
    LNG = sb("LNG", [128, 64])
    LNB = sb("LNB", [128, 64])
    SP8 = sb("SP8", [128, 16])
    CF = sb("CF", [128, 1])
    G2T = sb("G2T", [128, 2, 8, 2])
    B2T = sb("B2T", [128, 2, 8, 2])
    ST = sb("ST", [128, 96])
    SMALL = sb("SMALL", [128, 160])
    banks = [es.enter_context(nc.psum_tensor("pb%d" % i, [128, 512], F32)) for i in range(8)]

    bXS = [[Buf("XS%d_%d" % (k, b)) for b in range(NB)] for k in range(KC)]
    bHB = [[Buf("HB%d_%d" % (k, b)) for b in range(NB)] for k in range(KC)]
    bSlot = [Buf("slot%d" % i) for i in range(NSLOT)]
    bBank = [Buf("bank%d" % i) for i in range(8)]
    bPT = Buf("PT")
    bMOD = [Buf("MOD0"), Buf("MOD1"), Buf("MOD0"), Buf("MOD1")]
    bMOD[2] = bMOD[0]
    bMOD[3] = bMOD[1]
    bMA = Buf("MA")
    bSCB = Buf("SCB")
    bIDENT = Buf("IDENT")
    bONES = Buf("ONES")
    bLN = Buf("LNGB")
    bSP8 = Buf("SP8")
    bCF = Buf("CF")
    bG2 = [Buf("G2_0"), Buf("G2_1")]
    bST = Buf("ST")
    bHP = Buf("HP")

    st = {"bank": 0, "slot": 0, "alt": 0}
    deferred = []
    marks = []

    def mark(name):
        marks.append((name, len(P.q["pe"]), len(P.q["act"]), len(P.q["dve"])))

    hold = set()

    def next_bank():
        i = st["bank"]
        while i in hold:
            i = (i + 1) % 8
        st["bank"] = (i + 1) % 8
        return banks[i], bBank[i]

    def fenced(name):
        b = Buf(name)
        for e_ in ("pe", "act", "dve", "pool"):
            if P.q[e_]:
                b.r[e_] = P.q[e_][-1]
        return b

    def fenced_pe(name):
        b = Buf(name)
        if P.q["pe"]:
            b.r["pe"] = P.q["pe"][-1]
        return b

    def refence(b):
        for e_ in ("pe", "act", "dve", "pool"):
            if P.q[e_]:
                b.r[e_] = P.q[e_][-1]

    def next_slot():
        i = st["slot"]
        while i in st.get("slot_busy", ()):
            i = (i + 1) % NSLOT
        st["slot"] = (i + 1) % NSLOT
        return RING[:, i, :], bSlot[i]

    def alt():
        st["alt"] ^= 1
        return st["alt"]

    def MM(out, lhsT, rhs, start, stop, reads, writes):
        P.add("pe", lambda e: e.matmul(out, lhsT=lhsT, rhs=rhs, start=start, stop=stop), reads, writes)

    def TR(out, in_, ident, reads, writes):
        P.add("pe", lambda e: e.transpose(out, in_, ident), reads, writes)

    def ACT(out, in_, func, reads, writes, scale=1.0, bias=0.0):
        P.add("act", lambda e: e.activation(out=out, in_=in_, func=func, bias=bias, scale=scale), reads, writes)

    def ACOPY(out, in_, reads, writes):
        P.add("act", lambda e: e.copy(out=out, in_=in_), reads, writes)

    def AMUL(out, in_, c, reads, writes):
        P.add("act", lambda e: e.activation(out=out, in_=in_, func=AF.Copy, scale=c), reads, writes)

    def VCOPY(out, in_, reads, writes):
        P.add("dve", lambda e: e.tensor_copy(out=out, in_=in_), reads, writes)

    def TT(out, in0, in1, op, reads, writes):
        P.add("dve", lambda e: e.tensor_tensor(out=out, in0=in0, in1=in1, op=op), reads, writes)

    def TS(out, in0, s1, s2, op0, op1, reads, writes):
        if s2 is None:
            P.add("dve", lambda e: e.tensor_scalar(out=out, in0=in0, scalar1=s1, scalar2=None, op0=op0), reads, writes)
        else:
            P.add("dve", lambda e: e.tensor_scalar(out=out, in0=in0, scalar1=s1, scalar2=s2, op0=op0, op1=op1), reads, writes)

    def PTT(out, in0, in1, op, reads, writes):
        P.add("pool", lambda e: e.tensor_tensor(out=out, in0=in0, in1=in1, op=op), reads, writes)

    def PTS(out, in0, s1, s2, op0, op1, reads, writes):
        P.add("pool", lambda e: e.tensor_scalar(out=out, in0=in0, scalar1=s1, scalar2=s2, op0=op0, op1=op1), reads, writes)

    def STT(out, in0, scalar, in1, op0, op1, reads, writes):
        P.add("dve", lambda e: e.scalar_tensor_tensor(out=out, in0=in0, scalar=scalar, in1=in1, op0=op0, op1=op1), reads, writes)

    def DMA(eng, out, in_, reads, writes, is_out=False):
        return P.add(eng, lambda e: e.dma_start(out=out, in_=in_), reads, writes, dma=True, is_out=is_out)

    def blk(b):
        return slice(b * 512, (b + 1) * 512)

    def mcol(b):
        return 0 if b < 2 else 1

    def bigf(off, n):
        return BIG[:, off:off + n]

    def bigb(off, n_words):
        return BIG[:, off:off + n_words].bitcast(BF16)

    ZSQ = bigb(0, 512).rearrange("p (i n) -> p i n", i=2)
    MR = bigf(512, 2048).rearrange("p (b k n) -> p b k n", b=2, k=2)
    bZSQ = [Buf("ZSQ%d" % i) for i in range(2)]
    bMR = [Buf("MR0"), Buf("MR1")]
    MIX0 = 2560

    HID = BIG[:, :].bitcast(BF16).rearrange("p (f t) -> p f t", f=FC)
    bHID = [[Buf("HID%d_%d" % (f, b)) for b in range(NB)] for f in range(FC)]
    ALIAS_LNT = [bHID[f][b] for f in range(4) for b in range(NB)]

    P.add("pool", lambda e: e.memset(IDENT[:], 0.0), [], [bIDENT])
    P.add("pool", lambda e: e.affine_select(out=IDENT[:], in_=IDENT[:], pattern=[[-1, 128]],
                                            compare_op=ALU.not_equal, fill=1.0, base=0, channel_multiplier=1),
          [bIDENT], [bIDENT])
    P.add("dve", lambda e: e.memset(ONES[:], 1.0), [], [bONES])
    P.add("dve", lambda e: e.memset(ONESB[:], 1.0), [], [bONES])
    P.add("dve", lambda e: e.memset(ST[:], 0.0), [], [bST])

    PR = bigf(MIX0, 512).rearrange("p (g c) -> p g c", g=4)
    bPR = Buf("PR")
    DMA("sp", PR, params.rearrange("(g r) c -> r g c", r=128), [], [bPR] + ALIAS_LNT[0:0])
    DMA("sp", CF[:], cfd[:, :], [], [bCF])
    XT = bigf(MIX0 + 512, 12288).rearrange("p (t d) -> p t d", t=NT)
    bXT = [Buf("XT%d" % t) for t in range(NT)]
    for t in range(NT):
        DMA("sp", XT[:, t, :], xin[t * 128:(t + 1) * 128, :], [], [bXT[t]])

    bk, bb = next_bank()
    for g in range(4):
        TR(bk[:, g * 128:(g + 1) * 128], PR[:, g, :], IDENT[:], [bPR, bIDENT], [bb])
    VCOPY(PT[:], bk[:, 0:448], [bb], [bPT])

    TS(LNG[:, 0:56], PT[:, R_LNG:R_LNG + 56], ALPHA, None, ALU.mult, None, [bPT], [bLN])
    VCOPY(LNG[:, 56:64], PT[:, R_LNG + 56:R_LNG + 64], [bPT], [bLN])
    TS(LNB[:, 0:56], PT[:, R_LNB:R_LNB + 56], ALPHA, None, ALU.mult, None, [bPT], [bLN])
    VCOPY(LNB[:, 56:64], PT[:, R_LNB + 56:R_LNB + 64], [bPT], [bLN])
    bTMP = Buf("tmp16")
    ACT(SMALL[:, 0:16], PT[:, R_LAM:R_LAM + 16], AF.Exp, [bPT], [bTMP], scale=-1.0)
    ACT(SMALL[:, 0:16], SMALL[:, 0:16], AF.Ln, [bTMP], [bTMP], bias=1.0)
    TS(SP8[:], SMALL[:, 0:16], -8.0, None, ALU.mult, None, [bTMP], [bSP8])
    ACT(SCB[:], PT[:, R_COND:R_COND + 16], AF.Silu, [bPT], [bSCB])

    for b in range(NB):
        for kc in range(KC):
            bk, bb = next_bank()
            for i in range(4):
                t = 4 * b + i
                TR(bk[:, i * 128:(i + 1) * 128], XT[:, t, kc * 128:(kc + 1) * 128], IDENT[:], [bXT[t], bIDENT], [bb])
            if alt():
                AMUL(XS[:, kc, blk(b)], bk[:], ALPHA, [bb], [bXS[kc][b]])
            else:
                TS(XS[:, kc, blk(b)], bk[:], ALPHA, None, ALU.mult, None, [bb], [bXS[kc][b]])

    def ada_gen(l):
        wv = w_ada[l].rearrange("(kc p) n -> p kc n", p=128)
        for nb in range(12):
            sl, bs_ = next_slot()
            slv = sl.rearrange("p (k n) -> p k n", k=8)
            DMA("pool", slv, wv[:, :, nb * 512:(nb + 1) * 512], [], [bs_])
            bk2, bb2 = next_bank()
            for i in range(4):
                for kc in range(KC):
                    MM(bk2[:, 2 * i:2 * i + 2], slv[:, kc, i * 128:(i + 1) * 128], SCB[:, 2 * kc:2 * kc + 2],
                       kc == 0, kc == KC - 1, [bSCB, bs_], [bb2])
            n0 = nb * 4
            TT(MOD[:, l % 2, n0:n0 + 4, :], bk2[:, 0:8].rearrange("p (n m) -> p n m", m=2),
               PT[:, R_BADA + l * 48 + n0:R_BADA + l * 48 + n0 + 4].unsqueeze(2).broadcast_to([128, 4, 2]),
               ALU.add, [bb2, bPT], [bMOD[l]])
            if nb in (2, 3, 8, 9):
                TS(MOD[:, l % 2, n0:n0 + 4, :], MOD[:, l % 2, n0:n0 + 4, :], 1.0, 1.0 / ALPHA, ALU.add, ALU.mult,
                   [bMOD[l]], [bMOD[l]])
            yield

    def modsc(l, which, kc, m):
        n = which * 8 + kc
        return MOD[:, l % 2, n, m:m + 1]

    def make_g2b2(slot, l, which_sh, li_prev):
        scp = MOD[:, l % 2, (which_sh + 1) * 8:(which_sh + 2) * 8, :]
        sh = MOD[:, l % 2, which_sh * 8:(which_sh + 1) * 8, :]
        g = LNG[:, li_prev * 8:(li_prev + 1) * 8].unsqueeze(2).broadcast_to([128, 8, 2])
        bt = LNB[:, li_prev * 8:(li_prev + 1) * 8].unsqueeze(2).broadcast_to([128, 8, 2])
        TT(G2T[:, slot], scp, g, ALU.mult, [bMOD[l], bLN], [bG2[slot]])
        TT(B2T[:, slot], scp, bt, ALU.mult, [bMOD[l], bLN], [bG2[slot]])
        TT(B2T[:, slot], B2T[:, slot], sh, ALU.add, [bMOD[l], bG2[slot]], [bG2[slot]])

    def flush_deferred():
        for (x_, g_, b_, buf_) in deferred:
            TS(x_, x_, g_, b_, ALU.mult, ALU.add, [buf_, bLN], [buf_])
        del deferred[:]

    lnb = {}

    def v2(ap):
        return ap.rearrange("p (a n) -> p a n", a=2)

    def ln_ctx_big(b, alias=()):
        mi = b % 2
        return {"zsq": [ZSQ[:, 0, :], ZSQ[:, 1, :]], "bzsq": bZSQ, "mean": v2(MR[:, mi, 0, :]), "rstd": v2(MR[:, mi, 1, :]),
                "bmr": bMR[mi], "al": list(alias)}

    hbctx = {}

    def ln_ctx_hb(r):
        if r not in hbctx:
            hbctx[r] = ([Buf("hbZSQ%d_0" % r), Buf("hbZSQ%d_1" % r)], Buf("hbMR%d" % r))
        bz, bm = hbctx[r]
        return {"zsq": [HB[:, 4, blk(r)], HB[:, 5, blk(r)]], "bzsq": bz,
                "mean": HB[:, 0:2, blk(r)].bitcast(F32), "rstd": HB[:, 2:4, blk(r)].bitcast(F32),
                "bmr": bm, "al": [bHB[k_][r] for k_ in range(6)]}

    def ln_sq(li, b, cx):
        al = cx["al"]
        s1, bs1 = next_bank()
        s2, bs2 = next_bank()
        h1, h2 = banks.index(s1), banks.index(s2)
        hold.add(h1)
        hold.add(h2)
        lnb[b] = (s1, bs1, s2, bs2, h1, h2)
        for kc in range(KC):
            zi = kc % 2
            zq = cx["zsq"][zi]
            bz = cx["bzsq"][zi]
            ACT(zq, XS[:, kc, blk(b)], AF.Square, [bXS[kc][b]] + al, [bz] + al)
            MM(s1[:], ONES[:], XS[:, kc, blk(b)], kc == 0, kc == KC - 1, [bONES, bXS[kc][b]], [bs1])
            MM(s2[:], ONESB[:], zq, kc == 0, kc == KC - 1, [bONES, bz] + al, [bs2])

    def ln_rstd(li, b, cx):
        al = cx["al"]
        s1, bs1, s2, bs2, h1, h2 = lnb.pop(b)
        hold.discard(h1)
        hold.discard(h2)
        mean = cx["mean"]
        rstd = cx["rstd"]
        bm = cx["bmr"]
        TS(mean, v2(s1[:]), 1.0 / D, None, ALU.mult, None, [bs1] + al, [bm] + al)
        TT(rstd, mean, mean, ALU.mult, [bm] + al, [bm] + al)
        STT(rstd, v2(s2[:]), 1.0 / D, rstd, ALU.mult, ALU.subtract, [bs2, bm] + al, [bm] + al)
        ACT(rstd, rstd, AF.Sqrt, [bm] + al, [bm] + al, bias=EPS)
        P.add("dve", lambda e: e.reciprocal(out=rstd, in_=rstd), [bm] + al, [bm] + al)

    def ln_norm(li, b, g2slot, cx, kcs=None):
        last = (li == 7)
        al = cx["al"]
        mean = cx["mean"]
        rstd = cx["rstd"]
        bm = cx["bmr"]
        m = mcol(b)
        kl = list(range(KC) if kcs is None else kcs)
        for g0 in range(0, len(kl), 4):
            grp = kl[g0:g0 + 4]
            k0, k1 = grp[0], grp[-1] + 1
            xg = XS[:, k0:k1, blk(b)].rearrange("p k (a n) -> p k a n", a=2)
            shp = [128, k1 - k0, 2, 256]
            bx = [bXS[k_][b] for k_ in grp]
            TT(xg, xg, mean.unsqueeze(1).broadcast_to(shp), ALU.subtract, bx + [bm] + al, bx)
            TT(xg, xg, rstd.unsqueeze(1).broadcast_to(shp), ALU.mult, bx + [bm] + al, bx)
        for kc in kl:
            x = XS[:, kc, blk(b)]
            if not last:
                ACT(HB[:, kc, blk(b)], x, AF.Identity, [bXS[kc][b], bG2[g2slot]], [bHB[kc][b]],
                    scale=G2T[:, g2slot, kc, m:m + 1], bias=B2T[:, g2slot, kc, m:m + 1])
            deferred.append((x, LNG[:, li * 8 + kc:li * 8 + kc + 1], LNB[:, li * 8 + kc:li * 8 + kc + 1], bXS[kc][b]))

    def zstep(bk, bb, l, which_g, dc, b):
        STT(XS[:, dc, blk(b)], bk[:], modsc(l, which_g, dc, mcol(b)), XS[:, dc, blk(b)], ALU.mult, ALU.add,
            [bb, bMOD[l], bXS[dc][b]], [bXS[dc][b]])

    pre_wout = {}

    def load_wout(l, wdram):
        wv = wdram.rearrange("(kc p) d -> p kc d", p=128)
        sls = []
        for h in range(2):
            sl, bs_ = next_slot()
            slv = sl.rearrange("p (k n) -> p k n", k=8)
            DMA("pool", slv, wv[:, :, h * 512:(h + 1) * 512], [], [bs_])
            sls.append((slv, bs_))
        pre_wout[l] = sls

    def out_proj_and_ln(l, wdram, IN, bIN, li, g2slot):
        if l not in pre_wout:
            load_wout(l, wdram)
        sls = pre_wout.pop(l)

        def proj(b):
            for dc in range(KC):
                slv, bs_ = sls[dc // 4]
                bk, bb = next_bank()
                for kc in range(KC):
                    MM(bk[:], slv[:, kc, (dc % 4) * 128:(dc % 4 + 1) * 128], IN[:, kc, blk(b)], kc == 0, kc == KC - 1,
                       [bs_, bIN[kc][b]], [bb])
                zstep(bk, bb, l, 2, dc, b)

        c0, c1, c2 = ln_ctx_big(0), ln_ctx_big(1), ln_ctx_big(2)
        proj(0)
        proj(1)
        ln_sq(li, 0, c0)
        ln_rstd(li, 0, c0)
        proj(2)
        ln_sq(li, 1, c1)
        ln_norm(li, 0, g2slot, c0)
        ln_rstd(li, 1, c1)
        ln_sq(li, 2, c2)
        ln_norm(li, 1, g2slot, c1)
        ln_rstd(li, 2, c2)
        ln_norm(li, 2, g2slot, c2)

    def run_interleaved(gens):
        gens = list(gens)
        while gens:
            for g_ in list(gens):
                try:
                    next(g_)
                except StopIteration:
                    gens.remove(g_)

    pre = {}

    def even_pre(l, j, nq=4):
        o = MIX0 + 6144
        GWT = bigb(o, 1024).rearrange("p (g c n) -> p g c n", g=4, c=4)
        bGW = [fenced("GWT%d" % gd) for gd in range(4)]
        wv = w_in_ab[j].rearrange("(kc p) n -> p kc n", p=128)
        slots = []
        for q in range(nq):
            sl, bs_ = next_slot()
            slv = sl.rearrange("p (k n) -> p k n", k=8)
            DMA("pool", slv, wv[:, :, q * 512:(q + 1) * 512], [], [bs_])
            slots.append((slv, bs_))
        for gate, wd in enumerate((lru_wa, lru_wx)):
            for d_ in range(2):
                gd = d_ * 2 + gate
                P.add("dve", (lambda g_: (lambda e: e.memset(GWT[:, g_, :, :].rearrange("p c n -> p (c n)"), 0.0)))(gd), [], [bGW[gd]])
                bpar = [Buf("GWp%d_%d" % (gd, par)) for par in range(2)]
                for par in range(2):
                    src = wd[j, d_, par::2].rearrange("c i k -> i c k")
                    bpar[par].w = bGW[gd].w
                    DMA("pool", GWT[par * 64:(par + 1) * 64, gd, :, par * 64:(par + 1) * 64], src, [], [bpar[par]])
                bGW[gd] = bpar
        pre[l] = (slots, bGW)

    def even_mixer(l, j):
        o = MIX0
        YAB = bigb(o, 6144).rearrange("p (k t) -> p k t", k=8); o += 6144
        bYAB = [[fenced("YAB%d_%d" % (k, b)) for b in range(NB)] for k in range(KC)]
        o_shared = o
        if l not in pre:
            even_pre(l, j, nq=2 if l == 0 else 4)
        slots, bGW = pre.pop(l)
        if l == 0:
            st["slot_busy"] = set(i_ for i_ in range(NSLOT) if any(bSlot[i_] is sb_ for (_, sb_) in slots))
        GWT = bigb(o, 1024).rearrange("p (g c n) -> p g c n", g=4, c=4); o += 1024
        XAP = bigf(o, 1554).rearrange("p (s n) -> p s n", s=NSEG); o += 1560
        XC2 = [bigf(o, 1536), bigf(o + 1536, 1536)]; o += 3072
        XCB = bigb(o, 768); o += 768
        RAd = [bigf(o, 1536), bigf(o + 1536, 1536)]; o += 3072
        IGd = [bigf(o, 1536), bigf(o + 1536, 1536)]; o += 3072
        MUd = [bigf(o, 1536), bigf(o + 1536, 1536)]; o += 3072
        assert o <= 24576, o
        bXAP = fenced("XAP"); bXC2 = [fenced("XC0"), fenced("XC1")]; bXCB = fenced("XCB")
        bRAd = [fenced("RA0"), fenced("RA1")]; bIGd = [fenced("IG0"), fenced("IG1")]; bMUd = [fenced("MU0"), fenced("MU1")]
        bINI = [Buf("INI0"), Buf("INI1")]

        P.add("dve", lambda e: e.memset(XAP.rearrange("p s n -> p (s n)"), 0.0), [], [bXAP])
        flush_deferred()

        def seg(ap):
            return ap.rearrange("p (s n) -> p s n", s=NSEG)

        stage = {}

        def prefix(c4):
            csl = slice(c4 * 128, (c4 + 1) * 128)
            XC = XC2[c4 % 2]
            bXC = bXC2[c4 % 2]
            XCv = seg(XC)
            for b in range(NB):
                bk, bb = next_bank()
                for kc in range(KC):
                    MM(bk[:], slots[0][0][:, kc, csl], HB[:, kc, blk(b)], kc == 0, kc == KC - 1,
                       [slots[0][1], bHB[kc][b]], [bb])
                ACOPY(XAP[:, 2 * b:2 * b + 2, 2:258], bk[:].rearrange("p (s n) -> p s n", s=2), [bb], [bXAP])
                yield
            TS(XAP[:, 1:4, 0:2], XAP[:, 0:3, 256:258], CF[:, 0:1], None, ALU.mult, None, [bXAP, bCF], [bXAP])
            TS(XAP[:, 0:3, 258:259], XAP[:, 1:4, 2:3], CF[:, 0:1], None, ALU.mult, None, [bXAP, bCF], [bXAP])
            yield

            def cw(k):
                r = R_CONVW + (j * 4 + k) * 4 + c4
                return PT[:, r:r + 1]
            cb = PT[:, R_CONVB + j * 4 + c4:R_CONVB + j * 4 + c4 + 1]
            TS(XCv, XAP[:, :, 2:258], cw(2), cb, ALU.mult, ALU.add, [bXAP, bPT], [bXC])
            yield
            STT(XCv, XAP[:, :, 0:256], cw(0), XCv, ALU.mult, ALU.add, [bXAP, bPT, bXC], [bXC])
            yield
            STT(XCv, XAP[:, :, 1:257], cw(1), XCv, ALU.mult, ALU.add, [bXAP, bPT, bXC], [bXC])
            yield
            STT(XCv, XAP[:, :, 3:259], cw(3), XCv, ALU.mult, ALU.add, [bXAP, bPT, bXC], [bXC])
            yield
            while c4 > 0 and stage.get(("gates", c4 - 1), 0) < 2:
                yield
            ACOPY(XCB, XC, [bXC], [bXCB])
            yield

        rb_store = {}

        def gates(c4, d_):
            IG = IGd[d_]
            bIG = bIGd[d_]
            rbanks = []
            rb_store[(c4, d_)] = rbanks
            gd = d_ * 2
            hb_ = SMALL[:, 96 + (j * 2 + d_) * 4 + c4:96 + (j * 2 + d_) * 4 + c4 + 1]
            for b in range(NB):
                bk, bb = next_bank()
                MM(bk[:], GWT[:, gd, c4, :], XCB[:, blk(b)], True, True, bGW[gd] + [bXCB], [bb])
                hold.add(banks.index(bk))
                rbanks.append((bk, bb))
                ACT(bk[:], bk[:], AF.Tanh, [bb, bHP], [bb], scale=0.5, bias=hb_)
                yield

        def chain(c4, d_):
            XC = XC2[c4 % 2]
            bXC = bXC2[c4 % 2]
            RA, IG, MU = RAd[d_], IGd[d_], MUd[d_]
            bRA, bIG, bMU = bRAd[d_], bIGd[d_], bMUd[d_]
            rbanks = rb_store.pop((c4, d_))
            hs_ = SMALL[:, 128 + (j * 2 + d_) * 4 + c4:128 + (j * 2 + d_) * 4 + c4 + 1]
            for b in range(NB):
                bk, bb = rbanks[b]
                ACT(RA[:, blk(b)], bk[:], AF.Exp, [bb, bHP], [bRA], scale=hs_, bias=hs_)
                hold.discard(banks.index(bk))
            yield
            TS(RA, RA, 1.0, None, ALU.min, None, [bRA], [bRA])
            yield
            gd = d_ * 2 + 1
            hbi = SMALL[:, 96 + 16 + (j * 2 + d_) * 4 + c4:96 + 16 + (j * 2 + d_) * 4 + c4 + 1]
            for b in range(NB):
                bk, bb = next_bank()
                MM(bk[:], GWT[:, gd, c4, :], XCB[:, blk(b)], True, True, bGW[gd] + [bXCB], [bb])
                ACT(IG[:, blk(b)], bk[:], AF.Tanh, [bb, bHP], [bIG], scale=0.5, bias=hbi)
                yield
            stage[("gates", c4)] = stage.get(("gates", c4), 0) + 1
            while c4 > 0 and not stage.get(("tail", c4 - 1)):
                yield
            ACT(MU, RA, AF.Square, [bRA], [bMU])
            yield
            ACT(MU, MU, AF.Sqrt, [bMU], [bMU], scale=-0.25, bias=0.25)
            yield
            STT(IG, IG, 1.0, XC, ALU.add, ALU.mult, [bIG, bXC], [bIG])
            yield
            TT(MU, MU, IG, ALU.mult, [bMU, bIG], [bMU])
            yield
            Hv = seg(IG)
            RAv = seg(RA)
            h0c = PT[:, R_H0 + (j * 2 + d_) * 4 + c4:R_H0 + (j * 2 + d_) * 4 + c4 + 1]
            if d_ == 0:
                TS(RAv[:, 1:4, 0:1], RAv[:, 1:4, 0:1], CF[:, 0:1], None, ALU.mult, None, [bRA, bCF], [bRA])
                P.add("dve", (lambda a__: (lambda e: e.memset(a__, 0.0)))(RAv[:, 5, 0:1]), [], [bRA])
            else:
                TS(RAv[:, 0:3, 255:256], RAv[:, 0:3, 255:256], CF[:, 0:1], None, ALU.mult, None, [bRA, bCF], [bRA])
                P.add("dve", (lambda a__: (lambda e: e.memset(a__, 0.0)))(RAv[:, 4, 255:256]), [], [bRA])
            yield
            for (t0_, t1_, init, rd) in ((0, 1024, h0c, [bPT]), (1024, 1536, 0.0, [])):
                if d_ == 0:
                    o_, a_, u_ = IG[:, t0_:t1_], RA[:, t0_:t1_], MU[:, t0_:t1_]
                else:
                    o_, a_, u_ = IG[:, t0_:t1_][:, ::-1], RA[:, t0_:t1_][:, ::-1], MU[:, t0_:t1_][:, ::-1]
                P.add("dve", (lambda o__, a__, u__, i__: (lambda e: e.tensor_tensor_scan(
                    out=o__, data0=a__, data1=u__, initial=i__, op0=ALU.mult, op1=ALU.add)))(o_, a_, u_, init),
                    [bRA, bMU] + rd, [bIG])
                yield
            scol = (j * 2 + d_) * 4 + c4
            if d_ == 0:
                VCOPY(ST[:, scol::16], Hv[:, :, 255], [bIG], [bST])
            else:
                VCOPY(ST[:, scol::16], Hv[:, :, 0], [bIG], [bST])
            yield

        def hsum(c4):
            TT(MUd[0], IGd[0], IGd[1], ALU.add, [bIGd[0], bIGd[1]], [bMUd[0]])

        def tail(c4):
            csl = slice(c4 * 128, (c4 + 1) * 128)
            for b in range(NB):
                bk, bb = next_bank()
                for kc in range(KC):
                    MM(bk[:], slots[1][0][:, kc, csl], HB[:, kc, blk(b)], kc == 0, kc == KC - 1,
                       [slots[1][1], bHB[kc][b]], [bb])
                ACT(bk[:], bk[:], AF.Gelu_apprx_tanh, [bb], [bb])
                TT(YAB[:, c4, blk(b)], MUd[0][:, blk(b)], bk[:], ALU.mult, [bMUd[0], bb], [bYAB[c4][b]])
                if b == NB - 1:
                    stage[("tail", c4)] = True
                yield

        if j == 0:
            TS(SMALL[:, 96:128], PT[:, R_BA:R_BA + 32], 0.5, None, ALU.mult, None, [bPT], [bHP])
            TS(SMALL[:, 128:144], SP8[:], 0.5, None, ALU.mult, None, [bSP8], [bHP])

        run_interleaved([prefix(0)])
        for c4 in range(4):
            run_interleaved([gates(c4, 0), gates(c4, 1)])
            gens = [chain(c4, 0), chain(c4, 1)]
            if c4 > 0:
                gens.append(tail(c4 - 1))
            if c4 < 3:
                gens.append(prefix(c4 + 1))
            if l == 0:
                gens.append(itertools.islice(agen0, 3))
            run_interleaved(gens)
            hsum(c4)
            if c4 == 2 and len(slots) < 4:
                for _ in agen0:
                    pass
                wv_ = w_in_ab[j].rearrange("(kc p) n -> p kc n", p=128)
                for q in range(len(slots), 4):
                    sl, bs_ = next_slot()
                    slv = sl.rearrange("p (k n) -> p k n", k=8)
                    DMA("pool", slv, wv_[:, :, q * 512:(q + 1) * 512], [], [bs_])
                    slots.append((slv, bs_))
                st["slot_busy"] = set()
        run_interleaved([tail(3)])

        mark("  sgu%d" % l)
        load_wout(l, w_out_ab[j])
        o = o_shared
        V = bigb(o, 3072).rearrange("p (t n) -> p t n", t=NT); o += 3072
        U = bigf(o, 1536); o += 1536
        VG = bigf(o, 6144).rearrange("p (i n) -> p i n", i=NT); o += 6144
        SMV = bigf(o, 128); o += 128
        BROW = bigf(o, 512); o += 512
        RS = bigf(o, 512).rearrange("p (g n) -> p g n", g=4); o += 512
        RT = bigf(o, 512).rearrange("p (g n) -> p g n", g=4); o += 512
        TMP = bigf(o, 1024).rearrange("p (i n) -> p i n", i=2); o += 1024
        BST = bigf(o, 512).rearrange("p (g n) -> p g n", g=4); o += 512
        WSS = bigf(o, 512).rearrange("p (g n) -> p g n", g=4); o += 512
        WST = bigb(o, 256).rearrange("p (g n) -> p g n", g=4); o += 256
        assert o <= 24576, o
        bV = [fenced("V%d" % t) for t in range(NT)]
        bU = fenced("U"); bVG = [fenced("VG%d" % t) for t in range(NT)]; bSMV = fenced("SMV"); bBROW = fenced("BROW"); bBST = fenced("BST"); bWSS = fenced("WSS"); bWST = fenced("WST")
        bRS = fenced("RS"); bRT = fenced("RT"); bTMP = [fenced("TMP0"), fenced("TMP1")]

        DMA("sp", BROW[0:1, :], sgu_ln_b[j:j + 1, :], [], [bBROW])
        DMA("sp", BST[0:1, :, :], sgu_bs[j:j + 1, :, :], [], [bBST])
        DMA("sp", WSS, sgu_ws[j].rearrange("g p q -> p g q"), [], [bWSS])

        STT6 = SMV[:, 0:72].rearrange("p (t k) -> p t k", t=NT)
        MVT = SMV[:, 72:96].rearrange("p (t k) -> p t k", t=NT)
        SDT = SMV[:, 96:108]
        for t in range(NT):
            b = t // 4
            bk, bb = next_bank()
            for kc in range(KC):
                MM(bk[:], HB[:, kc, t * 128:(t + 1) * 128], slots[3][0][:, kc, :], kc == 0, kc == KC - 1,
                   [slots[3][1], bHB[kc][b]], [bb])
            vg = VG[:, t, :]
            ACT(vg, bk[:], AF.Gelu_apprx_tanh, [bb], [bVG[t]])
            P.add("dve", (lambda o_, i_: (lambda e: e.bn_stats(out=o_, in_=i_)))(STT6[:, t, :], vg), [bVG[t]], [bSMV])
            P.add("dve", (lambda o_, i_: (lambda e: e.bn_aggr(out=o_, in_=i_)))(MVT[:, t, :], STT6[:, t, :]), [bSMV], [bSMV])
        bk, bb = next_bank()
        for g in range(4):
            TR(bk[:, g * 128:(g + 1) * 128], WSS[:, g, :], IDENT[:], [bWSS, bIDENT], [bb])
        ACOPY(WST.rearrange("p g n -> p (g n)"), bk[:], [bb], [bWST])
        bk, bb = next_bank()
        for g in range(4):
            MM(bk[0:1, g * 128:(g + 1) * 128], ONESB[:, 0:1], WST[:, g, :], True, True, [bONES, bWST], [bb])
        ACOPY(RS[0:1, :, :].rearrange("p g n -> p (g n)"), bk[0:1, :], [bb], [bRS])
        bk, bb = next_bank()
        for g in range(4):
            MM(bk[:, g * 128:(g + 1) * 128], BROW[0:1, g * 128:(g + 1) * 128], RS[0:1, g, :], True, False, [bBROW, bRS], [bb])
            MM(bk[:, g * 128:(g + 1) * 128], ONES[0:1, :], BST[0:1, g, :], False, True, [bONES, bBST], [bb])
        VCOPY(RT.rearrange("p g n -> p (g n)"), bk[:], [bb], [bRT])
        ACT(SDT, MVT[:, :, 1], AF.Sqrt, [bSMV], [bSMV], bias=EPS)
        P.add("dve", lambda e: e.reciprocal(out=SDT, in_=SDT), [bSMV], [bSMV])
        for t in range(NT):
            TS(V[:, t, :], VG[:, t, :], MVT[:, t, 0:1], SDT[:, t:t + 1], ALU.subtract, ALU.mult, [bVG[t], bSMV], [bV[t]])

        for g in range(4):
            gsl = slice(g * 128, (g + 1) * 128)
            gcol = PT[:, R_SGUG + j * 4 + g:R_SGUG + j * 4 + g + 1]
            for b in range(NB):
                bk, bb = next_bank()
                for kc in range(KC):
                    MM(bk[:], slots[2][0][:, kc, gsl], HB[:, kc, blk(b)], kc == 0, kc == KC - 1,
                       [slots[2][1], bHB[kc][b]], [bb])
                ACT(U[:, blk(b)], bk[:], AF.Gelu_apprx_tanh, [bb], [bU])
            for b in range(NB):
                bk, bb = next_bank()
                for i in range(4):
                    t = 4 * b + i
                    MM(bk[:, i * 128:(i + 1) * 128], V[:, t, gsl], WST[:, g, :], True, True, [bV[t], bWST], [bb])
                ti = (g * NB + b) % 2
                tmp = TMP[:, ti, :]
                STT(tmp.rearrange("p (i n) -> p i n", i=4), bk[:].rearrange("p (i n) -> p i n", i=4), gcol,
                    RT[:, g, :].unsqueeze(1).broadcast_to([128, 4, 128]), ALU.mult, ALU.add, [bb, bPT, bRT], [bTMP[ti]])
                TT(YAB[:, 4 + g, blk(b)], tmp, U[:, blk(b)], ALU.mult, [bTMP[ti], bU], [bYAB[4 + g][b]])

        mark("  oproj%d" % l)
        make_g2b2(0, l, 3, 2 * l)
        out_proj_and_ln(l, w_out_ab[j], YAB, bYAB, 2 * l, 0)

    def odd_mixer(l, j):
        o = MIX0
        FB = bigb(o, 6144).rearrange("p (k t) -> p k t", k=8); o += 6144
        Y = bigb(o, 8192).rearrange("p (s g n) -> p s g n", s=8, g=4); o += 8192
        PD1 = bigb(o, 4096).rearrange("p (c s n) -> p c s n", c=2, s=8); o += 4096
        PD2 = bigb(o, 2048).rearrange("p (c s n) -> p c s n", c=2, s=4); o += 2048
        CS = bigb(o, 512).rearrange("p (k n) -> p k n", k=2); o += 512
        assert o <= 24576
        bFB = [[fenced_pe("FB%d_%d" % (k, b)) for b in range(NB)] for k in range(KC)]
        bY = [fenced_pe("Y%d" % s) for s in range(8)]
        bPD1 = fenced_pe("PD1"); bPD2 = fenced_pe("PD2"); bCS = fenced_pe("CS")
        DMA("sp", CS, cs256.rearrange("(k p) n -> p k n", p=128), [], [bCS])
        pd1v = pd1.rearrange("c (s p) n -> p c s n", p=128)
        PD1b = RING[:, 2:4, :].rearrange("p i (c s n) -> p (i c) s n", c=1, s=8)
        st["slot"] = 0
        load_wout(l, w_out_c[j])
        DMA("sp", PD1, pd1v[:, :, :, 0:512], [], [bPD1])
        DMA("sp", PD1b, pd1v[:, :, :, 512:1024], [], [bSlot[2], bSlot[3]])
        DMA("sp", PD2, pd2.rearrange("c (s p) n -> p c s n", p=128), [], [bPD2])
        flush_deferred()

        def step1(t, ys):
            b = t // 4
            for g in range(4):
                bk, bb = next_bank()
                MM(bk[:], HB[:, 2 * g, t * 128:(t + 1) * 128], CS[:, 0, :], True, False, [bHB[2 * g][b], bCS], [bb])
                MM(bk[:], HB[:, 2 * g + 1, t * 128:(t + 1) * 128], CS[:, 1, :], False, True, [bHB[2 * g + 1][b], bCS], [bb])
                if alt():
                    ACOPY(Y[:, ys, g, :], bk[:], [bb], [bY[ys]])
                else:
                    VCOPY(Y[:, ys, g, :], bk[:], [bb], [bY[ys]])

        for t in range(8):
            step1(t, t)
        for tb in range(2):
            PDt = PD1 if tb == 0 else PD1b
            bPDt = [bPD1] if tb == 0 else [bSlot[2], bSlot[3]]
            for g in range(4):
                for kk in range(2):
                    bk, bb = next_bank()
                    n = 0
                    for s in range(8):
                        for c in range(2):
                            MM(bk[:], Y[:, s, g, c * 256 + kk * 128:c * 256 + (kk + 1) * 128], PDt[:, c, s, :],
                               n == 0, n == 15, [bY[s]] + bPDt, [bb])
                            n += 1
                    if alt():
                        ACOPY(FB[:, 2 * g + kk, blk(tb)], bk[:], [bb], [bFB[2 * g + kk][tb]])
                    else:
                        VCOPY(FB[:, 2 * g + kk, blk(tb)], bk[:], [bb], [bFB[2 * g + kk][tb]])
        mark("  fnetU2_%d" % l)
        for t in range(8, 12):
            step1(t, t - 8)
        for g in range(4):
            for kk in range(2):
                bk, bb = next_bank()
                n = 0
                for s in range(4):
                    for c in range(2):
                        MM(bk[:], Y[:, s, g, c * 256 + kk * 128:c * 256 + (kk + 1) * 128], PD2[:, c, s, :],
                           n == 0, n == 7, [bY[s], bPD2], [bb])
                        n += 1
                if alt():
                    ACOPY(FB[:, 2 * g + kk, blk(2)], bk[:], [bb], [bFB[2 * g + kk][2]])
                else:
                    VCOPY(FB[:, 2 * g + kk, blk(2)], bk[:], [bb], [bFB[2 * g + kk][2]])

        mark("  oproj%d" % l)
        make_g2b2(0, l, 3, 2 * l)
        out_proj_and_ln(l, w_out_c[j], FB, bFB, 2 * l, 0)

    def out_block(b):
        flush_deferred()
        for i4 in range(4):
            t = 4 * b + i4
            oi = t % 2
            OTi = HB[:, 4 * oi:4 * oi + 4, 0:512].bitcast(F32)
            bo = [bHB[4 * oi + q][0] for q in range(4)]
            for half in range(2):
                bk, bb = next_bank()
                for q in range(4):
                    kc = half * 4 + q
                    TR(bk[:, q * 128:(q + 1) * 128], XS[:, kc, t * 128:(t + 1) * 128], IDENT[:], [bXS[kc][b], bIDENT], [bb])
                src = bk[:].rearrange("p (q n) -> p q n", q=2)
                if half == 0:
                    ACOPY(OTi[:, 0:2, :], src, [bb], bo)
                else:
                    VCOPY(OTi[:, 2:4, :], src, [bb], bo)
            DMA("sp", yout[t * 128:(t + 1) * 128, :].rearrange("r (q n) -> r q n", q=4), OTi, bo, [], is_out=True)

    def ffn(l):
        for f_ in range(FC):
            for b_ in range(NB):
                refence(bHID[f_][b_])
        w1v = ffn_w1[l].rearrange("(kc p) f -> p kc f", p=128)
        w2v = ffn_w2[l].rearrange("(fc p) d -> p fc d", p=128)
        agen = ada_gen(l + 1) if l + 1 < DEPTH else iter(())
        def w1_groups(slv, bs_, fq, bs):
            for b in bs:
                for fl in range(4):
                    fc = fq * 4 + fl
                    bk, bb = next_bank()
                    for kc in range(KC):
                        MM(bk[:], slv[:, kc, fl * 128:(fl + 1) * 128], HB[:, kc, blk(b)], kc == 0, kc == KC - 1,
                           [bs_, bHB[kc][b]], [bb])
                    hw = [bHID[fc][b]] + (bZSQ + bMR if fc < 4 else [])
                    ACT(HID[:, fc, blk(b)], bk[:], AF.Square, [bb], hw)
                    STT(HID[:, fc, blk(b)], bk[:], 0.0, HID[:, fc, blk(b)], ALU.is_gt, ALU.mult, [bb, bHID[fc][b]], [bHID[fc][b]])
                    bb.r.pop("act", None)

        def w1_load(fq):
            sl, bs_ = next_slot()
            slv = sl.rearrange("p (k n) -> p k n", k=8)
            DMA("pool", slv, w1v[:, :, fq * 512:(fq + 1) * 512], [], [bs_])
            return slv, bs_

        s0 = w1_load(0)
        w1_groups(s0[0], s0[1], 0, (0, 1))
        next(agen, None)
        s1_ = w1_load(1)
        w1_groups(s1_[0], s1_[1], 1, (0, 1))
        w1_groups(s0[0], s0[1], 0, (2,))
        w1_groups(s1_[0], s1_[1], 1, (2,))
        next(agen, None)
        for fq in range(2, 8):
            slv, bs_ = w1_load(fq)
            if fq == 3:
                flush_deferred()
            w1_groups(slv, bs_, fq, range(NB))
            next(agen, None)
        mark("  w2_%d" % l)
        li2 = 2 * l + 1

        def w2_load(dc):
            sl, bs_ = next_slot()
            slv = sl.rearrange("p (f n) -> p f n", f=FC)
            DMA("pool", slv, w2v[:, :, dc * 128:(dc + 1) * 128], [], [bs_])
            return slv, bs_

        def w2_group(slv, bs_, dc, b):
            bk, bb = next_bank()
            for fc in range(FC):
                MM(bk[:], slv[:, fc, :], HID[:, fc, blk(b)], fc == 0, fc == FC - 1, [bs_, bHID[fc][b]], [bb])
            zstep(bk, bb, l, 5, dc, b)

        for dc in range(4):
            slv, bs_ = w2_load(dc)
            for b in range(NB):
                w2_group(slv, bs_, dc, b)
            next(agen, None)
        for _ in agen:
            pass
        if l + 1 < DEPTH:
            make_g2b2(1, l + 1, 0, 2 * l + 1)
        res = [w2_load(dc) for dc in range(4, 8)]
        mark("  ln2_%d" % l)
        cx = [ln_ctx_hb(1), ln_ctx_hb(2), ln_ctx_big(2, alias=ALIAS_LNT)]
        lastl = (l == DEPTH - 1)
        for b in range(NB):
            for i, dc in enumerate(range(4, 8)):
                w2_group(res[i][0], res[i][1], dc, b)
                if b > 0:
                    ln_norm(li2, b - 1, 1, cx[b - 1], kcs=(2 * i, 2 * i + 1))
            ln_sq(li2, b, cx[b])
            if lastl and b == NB - 1:
                out_block(0)
                out_block(1)
            ln_rstd(li2, b, cx[b])
        if l + 1 < DEPTH and (l + 1) % 2 == 0:
            even_pre(l + 1, (l + 1) // 2)
        ln_norm(li2, 2, 1, cx[2])
        if lastl:
            out_block(2)

    mark("ada0")
    agen0 = ada_gen(0)
    for _ in range(4):
        next(agen0)
    for kc in range(KC):
        for m, (t0, t1, bl) in enumerate(((0, 1024, (0, 1)), (1024, 1536, (2,)))):
            rd = [bXS[kc][b] for b in bl] + [bMOD[0]]
            wr = [bHB[kc][b] for b in bl]
            if alt():
                ACT(HB[:, kc, t0:t1], XS[:, kc, t0:t1], AF.Identity, rd, wr, scale=modsc(0, 1, kc, m), bias=modsc(0, 0, kc, m))
            else:
                TS(HB[:, kc, t0:t1], XS[:, kc, t0:t1], modsc(0, 1, kc, m), modsc(0, 0, kc, m), ALU.mult, ALU.add, rd, wr)

    for l in range(DEPTH):
        mark("mixer%d" % l)
        if l % 2 == 0:
            even_mixer(l, l // 2)
        else:
            odd_mixer(l, l // 2)
        mark("ffn%d" % l)
        ffn(l)
    mark("out")

    flush_deferred()
    bk, bb = next_bank()
    TR(bk[0:96, 0:128], ST[:, :], IDENT[:], [bST, bIDENT], [bb])
    STO = SMALL[0:96, 0:128] if False else MA[0:96, 0:128]
    VCOPY(STO, bk[0:96, 0:128], [bb], [bMA])
    DMA("sp", sout[:, :], STO, [bMA], [], is_out=True)

    P.finish()
    stats = P.emit(nc, es)
    stats["marks"] = marks
    es.close()
    return nc, stats


_CACHE = {}


def _dft_tables():
    if "dft" in _CACHE:
        return _CACHE["dft"]
    c = np.arange(256)
    ang = 2.0 * np.pi * np.outer(c, c) / 256.0
    cs = np.concatenate([np.cos(ang), np.sin(ang)], axis=1).astype(np.float32)
    s = np.arange(1024)
    a1 = 2.0 * np.pi * np.outer(s, s) / 1024.0
    sc1 = 1.0 / (16.0 * 32.0)
    pd1_s = np.stack([np.cos(a1) * sc1, -np.sin(a1) * sc1]).astype(np.float32)
    a2 = 2.0 * np.pi * np.outer(c, c) / 256.0
    sc2 = 1.0 / (16.0 * 16.0)
    blkc = (np.cos(a2) * sc2).astype(np.float32)
    blks = (-np.sin(a2) * sc2).astype(np.float32)
    pd1_p = np.zeros((2, 1024, 1024), np.float32)
    for i in range(4):
        pd1_p[0, i * 256:(i + 1) * 256, i * 256:(i + 1) * 256] = blkc
        pd1_p[1, i * 256:(i + 1) * 256, i * 256:(i + 1) * 256] = blks
    pd2 = pd1_p[:, :512, :512].copy()
    bf = ml_dtypes.bfloat16
    out = (cs.astype(bf), pd1_s.astype(bf), pd1_p.astype(bf), pd2.astype(bf))
    _CACHE["dft"] = out
    return out


def _prompt_ids(core):
    if core < 4:
        return None, [2 * core, 2 * core + 1]
    base = 8 + 6 * (core - 4)
    return [base + i for i in range(4)], [base + 4, base + 5]


def kernel(x_prompt, x_sample, state_lru, c, c_ctx, w_ada, b_ada, w_in_ab, conv_w, conv_b,
           lru_wa, lru_ba, lru_wx, lru_bx, lru_lam, sgu_ln_g, sgu_ln_b, sgu_ws, sgu_bs,
           w_out_ab, w_out_c, ffn_w1, ffn_w2, ln_g, ln_b):
    f32 = np.float32
    x_prompt = np.asarray(x_prompt, f32)
    x_sample = np.asarray(x_sample, f32)
    state_lru = np.asarray(state_lru, f32)
    c = np.asarray(c, f32)
    c_ctx = np.asarray(c_ctx, f32)
    cs, pd1_s, pd1_p, pd2 = _dft_tables()

    if "nc" not in _CACHE:
        _CACHE["nc"] = build_program()
    nc, stats = _CACHE["nc"]

    shared = {
        "w_ada": np.ascontiguousarray(w_ada, f32), "w_in_ab": np.ascontiguousarray(w_in_ab, f32),
        "lru_wa": np.ascontiguousarray(lru_wa, f32), "lru_wx": np.ascontiguousarray(lru_wx, f32),
        "sgu_ln_g": np.ascontiguousarray(sgu_ln_g, f32), "sgu_ln_b": np.ascontiguousarray(sgu_ln_b, f32),
        "sgu_ws": np.ascontiguousarray(sgu_ws, f32), "sgu_bs": np.ascontiguousarray(sgu_bs, f32),
        "w_out_ab": np.ascontiguousarray(w_out_ab, f32), "w_out_c": np.ascontiguousarray(w_out_c, f32),
        "ffn_w1": np.ascontiguousarray(ffn_w1, f32), "ffn_w2": np.ascontiguousarray(ffn_w2, f32),
        "pd2": pd2, "cs256": cs,
    }
    base = np.zeros((NROWS, 128), f32)
    base[R_BADA:R_BADA + 192] = np.asarray(b_ada, f32).reshape(192, 128)
    base[R_LNG:R_LNG + 64] = np.asarray(ln_g, f32).reshape(64, 128)
    base[R_LNB:R_LNB + 64] = np.asarray(ln_b, f32).reshape(64, 128)
    base[R_CONVW:R_CONVW + 32] = np.asarray(conv_w, f32).reshape(32, 128)
    base[R_CONVB:R_CONVB + 8] = np.asarray(conv_b, f32).reshape(8, 128)
    base[R_BA:R_BA + 16] = np.asarray(lru_ba, f32).reshape(16, 128)
    base[R_BX:R_BX + 16] = np.asarray(lru_bx, f32).reshape(16, 128)
    base[R_LAM:R_LAM + 16] = np.asarray(lru_lam, f32).reshape(16, 128)
    base[R_SGUG:R_SGUG + 8] = np.asarray(sgu_ln_g, f32).reshape(8, 128)

    in_maps = []
    for core in range(8):
        u1, u2 = _prompt_ids(core)
        pr = base.copy()
        if u1 is None:
            xs = [x_sample[core]] + [x_prompt[i] for i in u2]
            cond0 = c[core]
            pr[R_H0:R_H0 + 16] = state_lru[core].reshape(16, 128)
            cf = np.ones((128, 1), f32)
            pd1 = pd1_s
        else:
            xs = [x_prompt[i] for i in u1 + u2]
            cond0 = c_ctx
            cf = np.zeros((128, 1), f32)
            pd1 = pd1_p
        cond = np.stack([cond0.reshape(8, 128), c_ctx.reshape(8, 128)], axis=1)
        pr[R_COND:R_COND + 16] = cond.reshape(16, 128)
        m = dict(shared)
        m["xin"] = np.ascontiguousarray(np.concatenate(xs, axis=0), f32)
        m["params"] = pr
        m["cf"] = cf
        m["pd1"] = pd1
        in_maps.append(m)

    res = run_bass_kernel_spmd(nc, in_maps, core_ids=list(range(8)))
    rs = res.results

    y_prompt = np.zeros((32, 256, D), f32)
    y_sample = np.zeros((4, 1024, D), f32)
    new_state = np.zeros((32, 2, 2, 512), f32)
    for core in range(8):
        y = np.asarray(rs[core]["yout"], f32)
        s = np.asarray(rs[core]["sout"], f32).reshape(6, 2, 2, 512)
        u1, u2 = _prompt_ids(core)
        if u1 is None:
            y_sample[core] = y[0:1024]
        else:
            for i, pid in enumerate(u1):
                y_prompt[pid] = y[i * 256:(i + 1) * 256]
                new_state[pid] = s[i]
        for i, pid in enumerate(u2):
            y_prompt[pid] = y[1024 + i * 256:1024 + (i + 1) * 256]
            new_state[pid] = s[4 + i]
    return (y_prompt, y_sample, new_state)
```

```python
import itertools
import numpy as np
from contextlib import ExitStack
import ml_dtypes
import concourse.bass as bass
import concourse.mybir as mybir
from concourse.bass_utils import run_bass_kernel_spmd

F32 = mybir.dt.float32
BF16 = mybir.dt.bfloat16
AF = mybir.ActivationFunctionType
ALU = mybir.AluOpType

D = 1024
T = 1536
KC = 8
NB = 3
NT = 12
NSEG = 6
DFF = 4096
FC = 32
DEPTH = 4
ALPHA = (2.0 * DEPTH) ** 0.25
EPS = 1e-5
NSLOT = 4

R_BADA = 0
R_LNG = 192
R_LNB = 256
R_CONVW = 320
R_CONVB = 352
R_BA = 360
R_BX = 376
R_LAM = 392
R_COND = 408
R_H0 = 424
R_SGUG = 440
NROWS = 512


class Buf:
    __slots__ = ("name", "w", "r", "rd")

    def __init__(self, name):
        self.name = name
        self.w = None
        self.r = {}
        self.rd = []


class Op:
    __slots__ = ("eng", "fn", "deps", "flag", "cnt", "dma", "sem", "semval")

    def __init__(self, eng, fn, dma):
        self.eng = eng
        self.fn = fn
        self.dma = dma
        self.deps = []
        self.flag = False
        self.cnt = 0
        self.sem = None
        self.semval = 0


class Prog:
    def __init__(self, n_dma_sems=40, n_sw=24):
        self.q = {"pe": [], "act": [], "dve": [], "pool": [], "sp": []}
        self.n_dma_sems = n_dma_sems
        self.dma_last = [None] * n_dma_sems
        self.dma_uses = [0] * n_dma_sems
        self.rng = {"pool": (0, n_sw), "sp": (n_sw, n_dma_sems)}
        self.dma_rr = {"pool": 0, "sp": n_sw}
        self.out_dmas = []

    def add(self, eng, fn, reads=(), writes=(), dma=False, is_out=False):
        op = Op(eng, fn, dma)
        deps = {}
        for b in reads:
            if b.w is not None:
                deps[id(b.w)] = b.w
        for b in writes:
            if b.w is not None:
                deps[id(b.w)] = b.w
            for r in b.r.values():
                deps[id(r)] = r
            for r in b.rd:
                deps[id(r)] = r
        if dma:
            k = self.dma_rr[eng]
            lo, hi = self.rng[eng]
            self.dma_rr[eng] = lo + (k + 1 - lo) % (hi - lo)
            prev = self.dma_last[k]
            if prev is not None:
                deps[id(prev)] = prev
            self.dma_uses[k] += 1
            op.sem = k
            op.semval = 16 * self.dma_uses[k]
            self.dma_last[k] = op
        dl = []
        for d in deps.values():
            if d is op:
                continue
            if eng == "pe" and (not dma) and d.eng == "pe" and not d.dma:
                continue
            dl.append(d)
            d.flag = True
        op.deps = dl
        for b in reads:
            if dma:
                b.rd.append(op)
            else:
                b.r[eng] = op
        for b in writes:
            b.w = op
            b.r = {}
            b.rd = []
        self.q[eng].append(op)
        if is_out:
            self.out_dmas.append(op)
        return op

    def finish(self):
        op = Op("sp", None, False)
        op.deps = list(self.out_dmas)
        for d in op.deps:
            d.flag = True
        self.q["sp"].append(op)
        for e, ops in self.q.items():
            c = 0
            for o in ops:
                if o.dma or o.fn is None:
                    continue
                if o.flag:
                    c += 1
                    o.cnt = c

    def emit(self, nc, es):
        esem = {e: es.enter_context(nc.semaphore("s_" + e)) for e in self.q}
        dsem = [es.enter_context(nc.semaphore("d_%d" % i)) for i in range(self.n_dma_sems)]
        block = es.enter_context(nc.Block())
        stats = {}

        def run(engname, e):
            waited = {}
            nw = 0
            for op in self.q[engname]:
                need = {}
                for d in op.deps:
                    if d.dma:
                        s, v, key = dsem[d.sem], d.semval, ("d", d.sem)
                    else:
                        s, v, key = esem[d.eng], d.cnt, ("e", d.eng)
                    if waited.get(key, 0) >= v:
                        continue
                    if key not in need or need[key][1] < v:
                        need[key] = (s, v)
                items = list(need.items())
                attach = None
                if op.fn is not None and items:
                    attach = items.pop()
                for key, (s, v) in items:
                    e.wait_ge(s, v)
                    waited[key] = v
                    nw += 1
                if op.fn is None:
                    continue
                ins = op.fn(e)
                if attach is not None:
                    key, (s, v) = attach
                    ins._wait_ge(s, v)
                    waited[key] = v
                if op.dma:
                    ins.then_inc(dsem[op.sem], 16)
                elif op.flag:
                    ins.then_inc(esem[engname], 1)
            stats[engname] = (len(self.q[engname]), nw)

        @block.tensor
        def _(e):
            run("pe", e)

        @block.scalar
        def _(e):
            run("act", e)

        @block.vector
        def _(e):
            run("dve", e)

        @block.gpsimd
        def _(e):
            run("pool", e)

        @block.sync
        def _(e):
            run("sp", e)

        return stats


def build_program():
    nc = bass.Bass("TRN2", target_bir_lowering=False)
    P = Prog()
    es = ExitStack()

    def din(name, shape, dt=F32):
        return nc.dram_tensor(name, list(shape), dt, kind="ExternalInput").ap()

    def dout(name, shape, dt=F32):
        return nc.dram_tensor(name, list(shape), dt, kind="ExternalOutput").ap()

    xin = din("xin", [T, D])
    params = din("params", [NROWS, 128])
    cfd = din("cf", [128, 1])
    pd1 = din("pd1", [2, 1024, 1024], BF16)
    pd2 = din("pd2", [2, 512, 512], BF16)
    cs256 = din("cs256", [256, 512], BF16)
    w_ada = din("w_ada", [4, D, 6 * D])
    w_in_ab = din("w_in_ab", [2, D, 2048])
    lru_wa = din("lru_wa", [2, 2, 8, 64, 64])
    lru_wx = din("lru_wx", [2, 2, 8, 64, 64])
    sgu_ln_g = din("sgu_ln_g", [2, 512])
    sgu_ln_b = din("sgu_ln_b", [2, 512])
    sgu_ws = din("sgu_ws", [2, 4, 128, 128])
    sgu_bs = din("sgu_bs", [2, 4, 128])
    w_out_ab = din("w_out_ab", [2, D, D])
    w_out_c = din("w_out_c", [2, D, D])
    ffn_w1 = din("ffn_w1", [4, D, DFF])
    ffn_w2 = din("ffn_w2", [4, DFF, D])
    yout = dout("yout", [T, D])
    sout = dout("sout", [96, 128])

    def sb(name, shape, dt=F32):
        return es.enter_context(nc.sbuf_tensor(name, list(shape), dt))

    XS = sb("XS", [128, KC, T])
    HB = sb("HB", [128, KC, T], BF16)
    BIG = sb("BIG", [128, 24576])
    RING = sb("RING", [128, NSLOT, 4096], BF16)
    PT = sb("PT", [128, 448])
    MOD = sb("MOD", [128, 2, 48, 2])
    MA = sb("MA", [128, 512])
    SCB = sb("SCB", [128, 16], BF16)
    IDENT = sb("IDENT", [128, 128])
    ONES = sb("ONES", [128, 128])
    ONESB = sb("ONESB", [128, 128], BF16)
    LNG = sb("LNG", [128, 64])
    LNB = sb("LNB", [128, 64])
    SP8 = sb("SP8", [128, 16])
    CF = sb("CF", [128, 1])
    G2T = sb("G2T", [128, 2, 8, 2])
    B2T = sb("B2T", [128, 2, 8, 2])
    ST = sb("ST", [128, 96])
    SMALL = sb("SMALL", [128, 160])
    banks = [es.enter_context(nc.psum_tensor("pb%d" % i, [128, 512], F32)) for i in range(8)]

    bXS = [[Buf("XS%d_%d" % (k, b)) for b in range(NB)] for k in range(KC)]
    bHB = [[Buf("HB%d_%d" % (k, b)) for b in range(NB)] for k in range(KC)]
    bSlot = [Buf("slot%d" % i) for i in range(NSLOT)]
    bBank = [Buf("bank%d" % i) for i in range(8)]
    bPT = Buf("PT")
    bMOD = [Buf("MOD0"), Buf("MOD1"), Buf("MOD0"), Buf("MOD1")]
    bMOD[2] = bMOD[0]
    bMOD[3] = bMOD[1]
    bMA = Buf("MA")
    bSCB = Buf("SCB")
    bIDENT = Buf("IDENT")
    bONES = Buf("ONES")
    bLN = Buf("LNGB")
    bSP8 = Buf("SP8")
    bCF = Buf("CF")
    bG2 = [Buf("G2_0"), Buf("G2_1")]
    bST = Buf("ST")
    bHP = Buf("HP")

    st = {"bank": 0, "slot": 0, "alt": 0}
    deferred = []
    marks = []

    def mark(name):
        marks.append((name, len(P.q["pe"]), len(P.q["act"]), len(P.q["dve"])))

    hold = set()

    def next_bank():
        i = st["bank"]
        while i in hold:
            i = (i + 1) % 8
        st["bank"] = (i + 1) % 8
        return banks[i], bBank[i]

    def fenced(name):
        b = Buf(name)
        for e_ in ("pe", "act", "dve", "pool"):
            if P.q[e_]:
                b.r[e_] = P.q[e_][-1]
        return b

    def fenced_pe(name):
        b = Buf(name)
        if P.q["pe"]:
            b.r["pe"] = P.q["pe"][-1]
        return b

    def refence(b):
        for e_ in ("pe", "act", "dve", "pool"):
            if P.q[e_]:
                b.r[e_] = P.q[e_][-1]

    def next_slot():
        i = st["slot"]
        while i in st.get("slot_busy", ()):
            i = (i + 1) % NSLOT
        st["slot"] = (i + 1) % NSLOT
        return RING[:, i, :], bSlot[i]

    def alt():
        st["alt"] ^= 1
        return st["alt"]

    def MM(out, lhsT, rhs, start, stop, reads, writes):
        P.add("pe", lambda e: e.matmul(out, lhsT=lhsT, rhs=rhs, start=start, stop=stop), reads, writes)

    def TR(out, in_, ident, reads, writes):
        P.add("pe", lambda e: e.transpose(out, in_, ident), reads, writes)

    def ACT(out, in_, func, reads, writes, scale=1.0, bias=0.0):
        P.add("act", lambda e: e.activation(out=out, in_=in_, func=func, bias=bias, scale=scale), reads, writes)

    def ACOPY(out, in_, reads, writes):
        P.add("act", lambda e: e.copy(out=out, in_=in_), reads, writes)

    def AMUL(out, in_, c, reads, writes):
        P.add("act", lambda e: e.activation(out=out, in_=in_, func=AF.Copy, scale=c), reads, writes)

    def VCOPY(out, in_, reads, writes):
        P.add("dve", lambda e: e.tensor_copy(out=out, in_=in_), reads, writes)

    def TT(out, in0, in1, op, reads, writes):
        P.add("dve", lambda e: e.tensor_tensor(out=out, in0=in0, in1=in1, op=op), reads, writes)

    def TS(out, in0, s1, s2, op0, op1, reads, writes):
        if s2 is None:
            P.add("dve", lambda e: e.tensor_scalar(out=out, in0=in0, scalar1=s1, scalar2=None, op0=op0), reads, writes)
        else:
            P.add("dve", lambda e: e.tensor_scalar(out=out, in0=in0, scalar1=s1, scalar2=s2, op0=op0, op1=op1), reads, writes)

    def PTT(out, in0, in1, op, reads, writes):
        P.add("pool", lambda e: e.tensor_tensor(out=out, in0=in0, in1=in1, op=op), reads, writes)

    def PTS(out, in0, s1, s2, op0, op1, reads, writes):
        P.add("pool", lambda e: e.tensor_scalar(out=out, in0=in0, scalar1=s1, scalar2=s2, op0=op0, op1=op1), reads, writes)

    def STT(out, in0, scalar, in1, op0, op1, reads, writes):
        P.add("dve", lambda e: e.scalar_tensor_tensor(out=out, in0=in0, scalar=scalar, in1=in1, op0=op0, op1=op1), reads, writes)

    def DMA(eng, out, in_, reads, writes, is_out=False):
        return P.add(eng, lambda e: e.dma_start(out=out, in_=in_), reads, writes, dma=True, is_out=is_out)

    def blk(b):
        return slice(b * 512, (b + 1) * 512)

    def mcol(b):
        return 0 if b < 2 else 1

    def bigf(off, n):
        return BIG[:, off:off + n]

    def bigb(off, n_words):
        return BIG[:, off:off + n_words].bitcast(BF16)

    ZSQ = bigb(0, 512).rearrange("p (i n) -> p i n", i=2)
    MR = bigf(512, 2048).rearrange("p (b k n) -> p b k n", b=2, k=2)
    bZSQ = [Buf("ZSQ%d" % i) for i in range(2)]
    bMR = [Buf("MR0"), Buf("MR1")]
    MIX0 = 2560

    HID = BIG[:, :].bitcast(BF16).rearrange("p (f t) -> p f t", f=FC)
    bHID = [[Buf("HID%d_%d" % (f, b)) for b in range(NB)] for f in range(FC)]
    ALIAS_LNT = [bHID[f][b] for f in range(4) for b in range(NB)]

    P.add("pool", lambda e: e.memset(IDENT[:], 0.0), [], [bIDENT])
    P.add("pool", lambda e: e.affine_select(out=IDENT[:], in_=IDENT[:], pattern=[[-1, 128]],
                                            compare_op=ALU.not_equal, fill=1.0, base=0, channel_multiplier=1),
          [bIDENT], [bIDENT])
    P.add("dve", lambda e: e.memset(ONES[:], 1.0), [], [bONES])
    P.add("dve", lambda e: e.memset(ONESB[:], 1.0), [], [bONES])
    P.add("dve", lambda e: e.memset(ST[:], 0.0), [], [bST])

    PR = bigf(MIX0, 512).rearrange("p (g c) -> p g c", g=4)
    bPR = Buf("PR")
    DMA("sp", PR, params.rearrange("(g r) c -> r g c", r=128), [], [bPR] + ALIAS_LNT[0:0])
    DMA("sp", CF[:], cfd[:, :], [], [bCF])
    XT = bigf(MIX0 + 512, 12288).rearrange("p (t d) -> p t d", t=NT)
    bXT = [Buf("XT%d" % t) for t in range(NT)]
    for t in range(NT):
        DMA("sp", XT[:, t, :], xin[t * 128:(t + 1) * 128, :], [], [bXT[t]])

    bk, bb = next_bank()
    for g in range(4):
        TR(bk[:, g * 128:(g + 1) * 128], PR[:, g, :], IDENT[:], [bPR, bIDENT], [bb])
    VCOPY(PT[:], bk[:, 0:448], [bb], [bPT])

    TS(LNG[:, 0:56], PT[:, R_LNG:R_LNG + 56], ALPHA, None, ALU.mult, None, [bPT], [bLN])
    VCOPY(LNG[:, 56:64], PT[:, R_LNG + 56:R_LNG + 64], [bPT], [bLN])
    TS(LNB[:, 0:56], PT[:, R_LNB:R_LNB + 56], ALPHA, None, ALU.mult, None, [bPT], [bLN])
    VCOPY(LNB[:, 56:64], PT[:, R_LNB + 56:R_LNB + 64], [bPT], [bLN])
    bTMP = Buf("tmp16")
    ACT(SMALL[:, 0:16], PT[:, R_LAM:R_LAM + 16], AF.Exp, [bPT], [bTMP], scale=-1.0)
    ACT(SMALL[:, 0:16], SMALL[:, 0:16], AF.Ln, [bTMP], [bTMP], bias=1.0)
    TS(SP8[:], SMALL[:, 0:16], -8.0, None, ALU.mult, None, [bTMP], [bSP8])
    ACT(SCB[:], PT[:, R_COND:R_COND + 16], AF.Silu, [bPT], [bSCB])

    for b in range(NB):
        for kc in range(KC):
            bk, bb = next_bank()
            for i in range(4):
                t = 4 * b + i
                TR(bk[:, i * 128:(i + 1) * 128], XT[:, t, kc * 128:(kc + 1) * 128], IDENT[:], [bXT[t], bIDENT], [bb])
            if alt():
                AMUL(XS[:, kc, blk(b)], bk[:], ALPHA, [bb], [bXS[kc][b]])
            else:
                TS(XS[:, kc, blk(b)], bk[:], ALPHA, None, ALU.mult, None, [bb], [bXS[kc][b]])

    def ada_gen(l):
        wv = w_ada[l].rearrange("(kc p) n -> p kc n", p=128)
        for nb in range(12):
            sl, bs_ = next_slot()
            slv = sl.rearrange("p (k n) -> p k n", k=8)
            DMA("pool", slv, wv[:, :, nb * 512:(nb + 1) * 512], [], [bs_])
            bk2, bb2 = next_bank()
            for i in range(4):
                for kc in range(KC):
                    MM(bk2[:, 2 * i:2 * i + 2], slv[:, kc, i * 128:(i + 1) * 128], SCB[:, 2 * kc:2 * kc + 2],
                       kc == 0, kc == KC - 1, [bSCB, bs_], [bb2])
            n0 = nb * 4
            TT(MOD[:, l % 2, n0:n0 + 4, :], bk2[:, 0:8].rearrange("p (n m) -> p n m", m=2),
               PT[:, R_BADA + l * 48 + n0:R_BADA + l * 48 + n0 + 4].unsqueeze(2).broadcast_to([128, 4, 2]),
               ALU.add, [bb2, bPT], [bMOD[l]])
            if nb in (2, 3, 8, 9):
                TS(MOD[:, l % 2, n0:n0 + 4, :], MOD[:, l % 2, n0:n0 + 4, :], 1.0, 1.0 / ALPHA, ALU.add, ALU.mult,
                   [bMOD[l]], [bMOD[l]])
            yield

    def modsc(l, which, kc, m):
        n = which * 8 + kc
        return MOD[:, l % 2, n, m:m + 1]

    def make_g2b2(slot, l, which_sh, li_prev):
        scp = MOD[:, l % 2, (which_sh + 1) * 8:(which_sh + 2) * 8, :]
        sh = MOD[:, l % 2, which_sh * 8:(which_sh + 1) * 8, :]
        g = LNG[:, li_prev * 8:(li_prev + 1) * 8].unsqueeze(2).broadcast_to([128, 8, 2])
        bt = LNB[:, li_prev * 8:(li_prev + 1) * 8].unsqueeze(2).broadcast_to([128, 8, 2])
        TT(G2T[:, slot], scp, g, ALU.mult, [bMOD[l], bLN], [bG2[slot]])
        TT(B2T[:, slot], scp, bt, ALU.mult, [bMOD[l], bLN], [bG2[slot]])
        TT(B2T[:, slot], B2T[:, slot], sh, ALU.add, [bMOD[l], bG2[slot]], [bG2[slot]])

    def flush_deferred():
        for (x_, g_, b_, buf_) in deferred:
            TS(x_, x_, g_, b_, ALU.mult, ALU.add, [buf_, bLN], [buf_])
        del deferred[:]

    lnb = {}

    def v2(ap):
        return ap.rearrange("p (a n) -> p a n", a=2)

    def ln_ctx_big(b, alias=()):
        mi = b % 2
        return {"zsq": [ZSQ[:, 0, :], ZSQ[:, 1, :]], "bzsq": bZSQ, "mean": v2(MR[:, mi, 0, :]), "rstd": v2(MR[:, mi, 1, :]),
                "bmr": bMR[mi], "al": list(alias)}

    hbctx = {}

    def ln_ctx_hb(r):
        if r not in hbctx:
            hbctx[r] = ([Buf("hbZSQ%d_0" % r), Buf("hbZSQ%d_1" % r)], Buf("hbMR%d" % r))
        bz, bm = hbctx[r]
        return {"zsq": [HB[:, 4, blk(r)], HB[:, 5, blk(r)]], "bzsq": bz,
                "mean": HB[:, 0:2, blk(r)].bitcast(F32), "rstd": HB[:, 2:4, blk(r)].bitcast(F32),
                "bmr": bm, "al": [bHB[k_][r] for k_ in range(6)]}

    def ln_sq(li, b, cx):
        al = cx["al"]
        s1, bs1 = next_bank()
        s2, bs2 = next_bank()
        h1, h2 = banks.index(s1), banks.index(s2)
        hold.add(h1)
        hold.add(h2)
        lnb[b] = (s1, bs1, s2, bs2, h1, h2)
        for kc in range(KC):
            zi = kc % 2
            zq = cx["zsq"][zi]
            bz = cx["bzsq"][zi]
            ACT(zq, XS[:, kc, blk(b)], AF.Square, [bXS[kc][b]] + al, [bz] + al)
            MM(s1[:], ONES[:], XS[:, kc, blk(b)], kc == 0, kc == KC - 1, [bONES, bXS[kc][b]], [bs1])
            MM(s2[:], ONESB[:], zq, kc == 0, kc == KC - 1, [bONES, bz] + al, [bs2])

    def ln_rstd(li, b, cx):
        al = cx["al"]
        s1, bs1, s2, bs2, h1, h2 = lnb.pop(b)
        hold.discard(h1)
        hold.discard(h2)
        mean = cx["mean"]
        rstd = cx["rstd"]
        bm = cx["bmr"]
        TS(mean, v2(s1[:]), 1.0 / D, None, ALU.mult, None, [bs1] + al, [bm] + al)
        TT(rstd, mean, mean, ALU.mult, [bm] + al, [bm] + al)
        STT(rstd, v2(s2[:]), 1.0 / D, rstd, ALU.mult, ALU.subtract, [bs2, bm] + al, [bm] + al)
        ACT(rstd, rstd, AF.Sqrt, [bm] + al, [bm] + al, bias=EPS)
        P.add("dve", lambda e: e.reciprocal(out=rstd, in_=rstd), [bm] + al, [bm] + al)

    def ln_norm(li, b, g2slot, cx, kcs=None):
        last = (li == 7)
        al = cx["al"]
        mean = cx["mean"]
        rstd = cx["rstd"]
        bm = cx["bmr"]
        m = mcol(b)
        kl = list(range(KC) if kcs is None else kcs)
        for g0 in range(0, len(kl), 4):
            grp = kl[g0:g0 + 4]
            k0, k1 = grp[0], grp[-1] + 1
            xg = XS[:, k0:k1, blk(b)].rearrange("p k (a n) -> p k a n", a=2)
            shp = [128, k1 - k0, 2, 256]
            bx = [bXS[k_][b] for k_ in grp]
            TT(xg, xg, mean.unsqueeze(1).broadcast_to(shp), ALU.subtract, bx + [bm] + al, bx)
            TT(xg, xg, rstd.unsqueeze(1).broadcast_to(shp), ALU.mult, bx + [bm] + al, bx)
        for kc in kl:
            x = XS[:, kc, blk(b)]
            if not last:
                ACT(HB[:, kc, blk(b)], x, AF.Identity, [bXS[kc][b], bG2[g2slot]], [bHB[kc][b]],
                    scale=G2T[:, g2slot, kc, m:m + 1], bias=B2T[:, g2slot, kc, m:m + 1])
            deferred.append((x, LNG[:, li * 8 + kc:li * 8 + kc + 1], LNB[:, li * 8 + kc:li * 8 + kc + 1], bXS[kc][b]))

    def zstep(bk, bb, l, which_g, dc, b):
        STT(XS[:, dc, blk(b)], bk[:], modsc(l, which_g, dc, mcol(b)), XS[:, dc, blk(b)], ALU.mult, ALU.add,
            [bb, bMOD[l], bXS[dc][b]], [bXS[dc][b]])

    pre_wout = {}
    w1pre = {}

    def load_wout(l, wdram):
        wv = wdram.rearrange("(kc p) d -> p kc d", p=128)
        sls = []
        for h in range(2):
            sl, bs_ = next_slot()
            slv = sl.rearrange("p (k n) -> p k n", k=8)
            DMA("pool", slv, wv[:, :, h * 512:(h + 1) * 512], [], [bs_])
            sls.append((slv, bs_))
        pre_wout[l] = sls

    def out_proj_and_ln(l, wdram, IN, bIN, li, g2slot):
        if l not in pre_wout:
            load_wout(l, wdram)
        sls = pre_wout.pop(l)
        w1v_ = ffn_w1[l].rearrange("(kc p) f -> p kc f", p=128)
        w1pre[l] = []
        for fq_ in range(2):
            sl, bs_ = next_slot()
            slv = sl.rearrange("p (k n) -> p k n", k=8)
            DMA("pool", slv, w1v_[:, :, fq_ * 512:(fq_ + 1) * 512], [], [bs_])
            w1pre[l].append((slv, bs_))

        def proj(b):
            for dc in range(KC):
                slv, bs_ = sls[dc // 4]
                bk, bb = next_bank()
                for kc in range(KC):
                    MM(bk[:], slv[:, kc, (dc % 4) * 128:(dc % 4 + 1) * 128], IN[:, kc, blk(b)], kc == 0, kc == KC - 1,
                       [bs_, bIN[kc][b]], [bb])
                zstep(bk, bb, l, 2, dc, b)

        c0, c1, c2 = ln_ctx_big(0), ln_ctx_big(1), ln_ctx_big(2)
        proj(0)
        proj(1)
        ln_sq(li, 0, c0)
        ln_rstd(li, 0, c0)
        proj(2)
        ln_sq(li, 1, c1)
        ln_norm(li, 0, g2slot, c0)
        ln_rstd(li, 1, c1)
        ln_sq(li, 2, c2)
        ln_norm(li, 1, g2slot, c1)
        ln_rstd(li, 2, c2)
        ln_norm(li, 2, g2slot, c2)

    def run_interleaved(gens):
        gens = list(gens)
        while gens:
            for g_ in list(gens):
                try:
                    next(g_)
                except StopIteration:
                    gens.remove(g_)

    pre = {}

    def even_pre(l, j, nq=4):
        o = MIX0 + 6144
        GWT = bigb(o, 1024).rearrange("p (g c n) -> p g c n", g=4, c=4)
        bGW = [fenced("GWT%d" % gd) for gd in range(4)]
        wv = w_in_ab[j].rearrange("(kc p) n -> p kc n", p=128)
        slots = []
        for q in range(nq):
            sl, bs_ = next_slot()
            slv = sl.rearrange("p (k n) -> p k n", k=8)
            DMA("pool", slv, wv[:, :, q * 512:(q + 1) * 512], [], [bs_])
            slots.append((slv, bs_))
        for gate, wd in enumerate((lru_wa, lru_wx)):
            for d_ in range(2):
                gd = d_ * 2 + gate
                P.add("dve", (lambda g_: (lambda e: e.memset(GWT[:, g_, :, :].rearrange("p c n -> p (c n)"), 0.0)))(gd), [], [bGW[gd]])
                bpar = [Buf("GWp%d_%d" % (gd, par)) for par in range(2)]
                for par in range(2):
                    src = wd[j, d_, par::2].rearrange("c i k -> i c k")
                    bpar[par].w = bGW[gd].w
                    DMA("pool", GWT[par * 64:(par + 1) * 64, gd, :, par * 64:(par + 1) * 64], src, [], [bpar[par]])
                bGW[gd] = bpar
        pre[l] = (slots, bGW)

    def even_mixer(l, j):
        o = MIX0
        YAB = bigb(o, 6144).rearrange("p (k t) -> p k t", k=8); o += 6144
        bYAB = [[fenced("YAB%d_%d" % (k, b)) for b in range(NB)] for k in range(KC)]
        o_shared = o
        if l not in pre:
            even_pre(l, j, nq=2 if l == 0 else 4)
        slots, bGW = pre.pop(l)
        if l == 0:
            st["slot_busy"] = set(i_ for i_ in range(NSLOT) if any(bSlot[i_] is sb_ for (_, sb_) in slots))
        GWT = bigb(o, 1024).rearrange("p (g c n) -> p g c n", g=4, c=4); o += 1024
        XAP = bigf(o, 1554).rearrange("p (s n) -> p s n", s=NSEG); o += 1560
        XC2 = [bigf(o, 1536), bigf(o + 1536, 1536)]; o += 3072
        XCB = bigb(o, 768); o += 768
        RAd = [bigf(o, 1536), bigf(o + 1536, 1536)]; o += 3072
        IGd = [bigf(o, 1536), bigf(o + 1536, 1536)]; o += 3072
        MUd = [bigf(o, 1536), bigf(o + 1536, 1536)]; o += 3072
        assert o <= 24576, o
        bXAP = fenced("XAP"); bXC2 = [fenced("XC0"), fenced("XC1")]; bXCB = fenced("XCB")
        bRAd = [fenced("RA0"), fenced("RA1")]; bIGd = [fenced("IG0"), fenced("IG1")]; bMUd = [fenced("MU0"), fenced("MU1")]
        bINI = [Buf("INI0"), Buf("INI1")]

        P.add("dve", lambda e: e.memset(XAP.rearrange("p s n -> p (s n)"), 0.0), [], [bXAP])
        flush_deferred()

        def seg(ap):
            return ap.rearrange("p (s n) -> p s n", s=NSEG)

        stage = {}

        def prefix(c4):
            csl = slice(c4 * 128, (c4 + 1) * 128)
            XC = XC2[c4 % 2]
            bXC = bXC2[c4 % 2]
            XCv = seg(XC)
            for b in range(NB):
                bk, bb = next_bank()
                for kc in range(KC):
                    MM(bk[:], slots[0][0][:, kc, csl], HB[:, kc, blk(b)], kc == 0, kc == KC - 1,
                       [slots[0][1], bHB[kc][b]], [bb])
                ACOPY(XAP[:, 2 * b:2 * b + 2, 2:258], bk[:].rearrange("p (s n) -> p s n", s=2), [bb], [bXAP])
                yield
            TS(XAP[:, 1:4, 0:2], XAP[:, 0:3, 256:258], CF[:, 0:1], None, ALU.mult, None, [bXAP, bCF], [bXAP])
            TS(XAP[:, 0:3, 258:259], XAP[:, 1:4, 2:3], CF[:, 0:1], None, ALU.mult, None, [bXAP, bCF], [bXAP])
            yield

            def cw(k):
                r = R_CONVW + (j * 4 + k) * 4 + c4
                return PT[:, r:r + 1]
            cb = PT[:, R_CONVB + j * 4 + c4:R_CONVB + j * 4 + c4 + 1]
            TS(XCv, XAP[:, :, 2:258], cw(2), cb, ALU.mult, ALU.add, [bXAP, bPT], [bXC])
            yield
            STT(XCv, XAP[:, :, 0:256], cw(0), XCv, ALU.mult, ALU.add, [bXAP, bPT, bXC], [bXC])
            yield
            STT(XCv, XAP[:, :, 1:257], cw(1), XCv, ALU.mult, ALU.add, [bXAP, bPT, bXC], [bXC])
            yield
            STT(XCv, XAP[:, :, 3:259], cw(3), XCv, ALU.mult, ALU.add, [bXAP, bPT, bXC], [bXC])
            yield
            while c4 > 0 and stage.get(("gates", c4 - 1), 0) < 2:
                yield
            ACOPY(XCB, XC, [bXC], [bXCB])
            yield

        rb_store = {}

        def gates(c4, d_):
            IG = IGd[d_]
            bIG = bIGd[d_]
            rbanks = []
            rb_store[(c4, d_)] = rbanks
            gd = d_ * 2
            hb_ = SMALL[:, 96 + (j * 2 + d_) * 4 + c4:96 + (j * 2 + d_) * 4 + c4 + 1]
            for b in range(NB):
                bk, bb = next_bank()
                MM(bk[:], GWT[:, gd, c4, :], XCB[:, blk(b)], True, True, bGW[gd] + [bXCB], [bb])
                hold.add(banks.index(bk))
                rbanks.append((bk, bb))
                ACT(bk[:], bk[:], AF.Tanh, [bb, bHP], [bb], scale=0.5, bias=hb_)
                yield

        def chain(c4, d_):
            XC = XC2[c4 % 2]
            bXC = bXC2[c4 % 2]
            RA, IG, MU = RAd[d_], IGd[d_], MUd[d_]
            bRA, bIG, bMU = bRAd[d_], bIGd[d_], bMUd[d_]
            rbanks = rb_store.pop((c4, d_))
            hs_ = SMALL[:, 128 + (j * 2 + d_) * 4 + c4:128 + (j * 2 + d_) * 4 + c4 + 1]
            for b in range(NB):
                bk, bb = rbanks[b]
                ACT(RA[:, blk(b)], bk[:], AF.Exp, [bb, bHP], [bRA], scale=hs_, bias=hs_)
                hold.discard(banks.index(bk))
            yield
            TS(RA, RA, 1.0, None, ALU.min, None, [bRA], [bRA])
            yield
            gd = d_ * 2 + 1
            hbi = SMALL[:, 96 + 16 + (j * 2 + d_) * 4 + c4:96 + 16 + (j * 2 + d_) * 4 + c4 + 1]
            for b in range(NB):
                bk, bb = next_bank()
                MM(bk[:], GWT[:, gd, c4, :], XCB[:, blk(b)], True, True, bGW[gd] + [bXCB], [bb])
                ACT(IG[:, blk(b)], bk[:], AF.Tanh, [bb, bHP], [bIG], scale=0.5, bias=hbi)
                yield
            stage[("gates", c4)] = stage.get(("gates", c4), 0) + 1
            while c4 > 0 and not stage.get(("tail", c4 - 1)):
                yield
            ACT(MU, RA, AF.Square, [bRA], [bMU])
            yield
            ACT(MU, MU, AF.Sqrt, [bMU], [bMU], scale=-0.25, bias=0.25)
            yield
            STT(IG, IG, 1.0, XC, ALU.add, ALU.mult, [bIG, bXC], [bIG])
            yield
            TT(MU, MU, IG, ALU.mult, [bMU, bIG], [bMU])
            yield
            Hv = seg(IG)
            RAv = seg(RA)
            h0c = PT[:, R_H0 + (j * 2 + d_) * 4 + c4:R_H0 + (j * 2 + d_) * 4 + c4 + 1]
            if d_ == 0:
                TS(RAv[:, 1:4, 0:1], RAv[:, 1:4, 0:1], CF[:, 0:1], None, ALU.mult, None, [bRA, bCF], [bRA])
                P.add("dve", (lambda a__: (lambda e: e.memset(a__, 0.0)))(RAv[:, 5, 0:1]), [], [bRA])
            else:
                TS(RAv[:, 0:3, 255:256], RAv[:, 0:3, 255:256], CF[:, 0:1], None, ALU.mult, None, [bRA, bCF], [bRA])
                P.add("dve", (lambda a__: (lambda e: e.memset(a__, 0.0)))(RAv[:, 4, 255:256]), [], [bRA])
            yield
            for (t0_, t1_, init, rd) in ((0, 1024, h0c, [bPT]), (1024, 1536, 0.0, [])):
                if d_ == 0:
                    o_, a_, u_ = IG[:, t0_:t1_], RA[:, t0_:t1_], MU[:, t0_:t1_]
                else:
                    o_, a_, u_ = IG[:, t0_:t1_][:, ::-1], RA[:, t0_:t1_][:, ::-1], MU[:, t0_:t1_][:, ::-1]
                P.add("dve", (lambda o__, a__, u__, i__: (lambda e: e.tensor_tensor_scan(
                    out=o__, data0=a__, data1=u__, initial=i__, op0=ALU.mult, op1=ALU.add)))(o_, a_, u_, init),
                    [bRA, bMU] + rd, [bIG])
                yield
            scol = (j * 2 + d_) * 4 + c4
            if d_ == 0:
                VCOPY(ST[:, scol::16], Hv[:, :, 255], [bIG], [bST])
            else:
                VCOPY(ST[:, scol::16], Hv[:, :, 0], [bIG], [bST])
            yield

        def hsum(c4):
            TT(MUd[0], IGd[0], IGd[1], ALU.add, [bIGd[0], bIGd[1]], [bMUd[0]])

        def tail(c4):
            csl = slice(c4 * 128, (c4 + 1) * 128)
            for b in range(NB):
                bk, bb = next_bank()
                for kc in range(KC):
                    MM(bk[:], slots[1][0][:, kc, csl], HB[:, kc, blk(b)], kc == 0, kc == KC - 1,
                       [slots[1][1], bHB[kc][b]], [bb])
                ACT(bk[:], bk[:], AF.Gelu_apprx_tanh, [bb], [bb])
                TT(YAB[:, c4, blk(b)], MUd[0][:, blk(b)], bk[:], ALU.mult, [bMUd[0], bb], [bYAB[c4][b]])
                if b == NB - 1:
                    stage[("tail", c4)] = True
                yield

        if j == 0:
            TS(SMALL[:, 96:128], PT[:, R_BA:R_BA + 32], 0.5, None, ALU.mult, None, [bPT], [bHP])
            TS(SMALL[:, 128:144], SP8[:], 0.5, None, ALU.mult, None, [bSP8], [bHP])

        run_interleaved([prefix(0)])
        for c4 in range(4):
            run_interleaved([gates(c4, 0), gates(c4, 1)])
            gens = [chain(c4, 0), chain(c4, 1)]
            if c4 > 0:
                gens.append(tail(c4 - 1))
            if c4 < 3:
                gens.append(prefix(c4 + 1))
            if l == 0:
                gens.append(itertools.islice(agen0, 3))
            run_interleaved(gens)
            hsum(c4)
            if c4 == 2 and len(slots) < 4:
                for _ in agen0:
                    pass
                wv_ = w_in_ab[j].rearrange("(kc p) n -> p kc n", p=128)
                for q in range(len(slots), 4):
                    sl, bs_ = next_slot()
                    slv = sl.rearrange("p (k n) -> p k n", k=8)
                    DMA("pool", slv, wv_[:, :, q * 512:(q + 1) * 512], [], [bs_])
                    slots.append((slv, bs_))
                st["slot_busy"] = set()
        run_interleaved([tail(3)])

        mark("  sgu%d" % l)
        load_wout(l, w_out_ab[j])
        o = o_shared
        V = bigb(o, 3072).rearrange("p (t n) -> p t n", t=NT); o += 3072
        U = bigf(o, 1536); o += 1536
        VG = bigf(o, 6144).rearrange("p (i n) -> p i n", i=NT); o += 6144
        SMV = bigf(o, 128); o += 128
        BROW = bigf(o, 512); o += 512
        RS = bigf(o, 512).rearrange("p (g n) -> p g n", g=4); o += 512
        RT = bigf(o, 512).rearrange("p (g n) -> p g n", g=4); o += 512
        TMP = bigf(o, 1024).rearrange("p (i n) -> p i n", i=2); o += 1024
        BST = bigf(o, 512).rearrange("p (g n) -> p g n", g=4); o += 512
        WSS = bigf(o, 512).rearrange("p (g n) -> p g n", g=4); o += 512
        WST = bigb(o, 256).rearrange("p (g n) -> p g n", g=4); o += 256
        assert o <= 24576, o
        bV = [fenced("V%d" % t) for t in range(NT)]
        bU = fenced("U"); bVG = [fenced("VG%d" % t) for t in range(NT)]; bSMV = fenced("SMV"); bBROW = fenced("BROW"); bBST = fenced("BST"); bWSS = fenced("WSS"); bWST = fenced("WST")
        bRS = fenced("RS"); bRT = fenced("RT"); bTMP = [fenced("TMP0"), fenced("TMP1")]

        DMA("sp", BROW[0:1, :], sgu_ln_b[j:j + 1, :], [], [bBROW])
        DMA("sp", BST[0:1, :, :], sgu_bs[j:j + 1, :, :], [], [bBST])
        DMA("sp", WSS, sgu_ws[j].rearrange("g p q -> p g q"), [], [bWSS])

        STT6 = SMV[:, 0:72].rearrange("p (t k) -> p t k", t=NT)
        MVT = SMV[:, 72:96].rearrange("p (t k) -> p t k", t=NT)
        SDT = SMV[:, 96:108]
        for t in range(NT):
            b = t // 4
            bk, bb = next_bank()
            for kc in range(KC):
                MM(bk[:], HB[:, kc, t * 128:(t + 1) * 128], slots[3][0][:, kc, :], kc == 0, kc == KC - 1,
                   [slots[3][1], bHB[kc][b]], [bb])
            vg = VG[:, t, :]
            ACT(vg, bk[:], AF.Gelu_apprx_tanh, [bb], [bVG[t]])
            P.add("dve", (lambda o_, i_: (lambda e: e.bn_stats(out=o_, in_=i_)))(STT6[:, t, :], vg), [bVG[t]], [bSMV])
            P.add("dve", (lambda o_, i_: (lambda e: e.bn_aggr(out=o_, in_=i_)))(MVT[:, t, :], STT6[:, t, :]), [bSMV], [bSMV])
        bk, bb = next_bank()
        for g in range(4):
            TR(bk[:, g * 128:(g + 1) * 128], WSS[:, g, :], IDENT[:], [bWSS, bIDENT], [bb])
        ACOPY(WST.rearrange("p g n -> p (g n)"), bk[:], [bb], [bWST])
        bk, bb = next_bank()
        for g in range(4):
            MM(bk[0:1, g * 128:(g + 1) * 128], ONESB[:, 0:1], WST[:, g, :], True, True, [bONES, bWST], [bb])
        ACOPY(RS[0:1, :, :].rearrange("p g n -> p (g n)"), bk[0:1, :], [bb], [bRS])
        bk, bb = next_bank()
        for g in range(4):
            MM(bk[:, g * 128:(g + 1) * 128], BROW[0:1, g * 128:(g + 1) * 128], RS[0:1, g, :], True, False, [bBROW, bRS], [bb])
            MM(bk[:, g * 128:(g + 1) * 128], ONES[0:1, :], BST[0:1, g, :], False, True, [bONES, bBST], [bb])
        VCOPY(RT.rearrange("p g n -> p (g n)"), bk[:], [bb], [bRT])
        ACT(SDT, MVT[:, :, 1], AF.Sqrt, [bSMV], [bSMV], bias=EPS)
        P.add("dve", lambda e: e.reciprocal(out=SDT, in_=SDT), [bSMV], [bSMV])
        for t in range(NT):
            TS(V[:, t, :], VG[:, t, :], MVT[:, t, 0:1], SDT[:, t:t + 1], ALU.subtract, ALU.mult, [bVG[t], bSMV], [bV[t]])

        for g in range(4):
            gsl = slice(g * 128, (g + 1) * 128)
            gcol = PT[:, R_SGUG + j * 4 + g:R_SGUG + j * 4 + g + 1]
            for b in range(NB):
                bk, bb = next_bank()
                for kc in range(KC):
                    MM(bk[:], slots[2][0][:, kc, gsl], HB[:, kc, blk(b)], kc == 0, kc == KC - 1,
                       [slots[2][1], bHB[kc][b]], [bb])
                ACT(U[:, blk(b)], bk[:], AF.Gelu_apprx_tanh, [bb], [bU])
            for b in range(NB):
                bk, bb = next_bank()
                for i in range(4):
                    t = 4 * b + i
                    MM(bk[:, i * 128:(i + 1) * 128], V[:, t, gsl], WST[:, g, :], True, True, [bV[t], bWST], [bb])
                ti = (g * NB + b) % 2
                tmp = TMP[:, ti, :]
                STT(tmp.rearrange("p (i n) -> p i n", i=4), bk[:].rearrange("p (i n) -> p i n", i=4), gcol,
                    RT[:, g, :].unsqueeze(1).broadcast_to([128, 4, 128]), ALU.mult, ALU.add, [bb, bPT, bRT], [bTMP[ti]])
                TT(YAB[:, 4 + g, blk(b)], tmp, U[:, blk(b)], ALU.mult, [bTMP[ti], bU], [bYAB[4 + g][b]])

        mark("  oproj%d" % l)
        make_g2b2(0, l, 3, 2 * l)
        out_proj_and_ln(l, w_out_ab[j], YAB, bYAB, 2 * l, 0)

    def odd_mixer(l, j):
        o = MIX0
        FB = bigb(o, 6144).rearrange("p (k t) -> p k t", k=8); o += 6144
        Y = bigb(o, 8192).rearrange("p (s g n) -> p s g n", s=8, g=4); o += 8192
        PD1 = bigb(o, 4096).rearrange("p (c s n) -> p c s n", c=2, s=8); o += 4096
        PD2 = bigb(o, 2048).rearrange("p (c s n) -> p c s n", c=2, s=4); o += 2048
        CS = bigb(o, 512).rearrange("p (k n) -> p k n", k=2); o += 512
        assert o <= 24576
        bFB = [[fenced_pe("FB%d_%d" % (k, b)) for b in range(NB)] for k in range(KC)]
        bY = [fenced_pe("Y%d" % s) for s in range(8)]
        bPD1 = fenced_pe("PD1"); bPD2 = fenced_pe("PD2"); bCS = fenced_pe("CS")
        DMA("sp", CS, cs256.rearrange("(k p) n -> p k n", p=128), [], [bCS])
        pd1v = pd1.rearrange("c (s p) n -> p c s n", p=128)
        PD1b = RING[:, 2:4, :].rearrange("p i (c s n) -> p (i c) s n", c=1, s=8)
        st["slot"] = 0
        load_wout(l, w_out_c[j])
        DMA("sp", PD1, pd1v[:, :, :, 0:512], [], [bPD1])
        DMA("sp", PD1b, pd1v[:, :, :, 512:1024], [], [bSlot[2], bSlot[3]])
        DMA("sp", PD2, pd2.rearrange("c (s p) n -> p c s n", p=128), [], [bPD2])
        flush_deferred()

        def step1(t, ys):
            b = t // 4
            for g in range(4):
                bk, bb = next_bank()
                MM(bk[:], HB[:, 2 * g, t * 128:(t + 1) * 128], CS[:, 0, :], True, False, [bHB[2 * g][b], bCS], [bb])
                MM(bk[:], HB[:, 2 * g + 1, t * 128:(t + 1) * 128], CS[:, 1, :], False, True, [bHB[2 * g + 1][b], bCS], [bb])
                if alt():
                    ACOPY(Y[:, ys, g, :], bk[:], [bb], [bY[ys]])
                else:
                    VCOPY(Y[:, ys, g, :], bk[:], [bb], [bY[ys]])

        for t in range(8):
            step1(t, t)
        for tb in range(2):
            PDt = PD1 if tb == 0 else PD1b
            bPDt = [bPD1] if tb == 0 else [bSlot[2], bSlot[3]]
            for g in range(4):
                for kk in range(2):
                    bk, bb = next_bank()
                    n = 0
                    for s in range(8):
                        for c in range(2):
                            MM(bk[:], Y[:, s, g, c * 256 + kk * 128:c * 256 + (kk + 1) * 128], PDt[:, c, s, :],
                               n == 0, n == 15, [bY[s]] + bPDt, [bb])
                            n += 1
                    if alt():
                        ACOPY(FB[:, 2 * g + kk, blk(tb)], bk[:], [bb], [bFB[2 * g + kk][tb]])
                    else:
                        VCOPY(FB[:, 2 * g + kk, blk(tb)], bk[:], [bb], [bFB[2 * g + kk][tb]])
        mark("  fnetU2_%d" % l)
        for t in range(8, 12):
            step1(t, t - 8)
        for g in range(4):
            for kk in range(2):
                bk, bb = next_bank()
                n = 0
                for s in range(4):
                    for c in range(2):
                        MM(bk[:], Y[:, s, g, c * 256 + kk * 128:c * 256 + (kk + 1) * 128], PD2[:, c, s, :],
                           n == 0, n == 7, [bY[s], bPD2], [bb])
                        n += 1
                if alt():
                    ACOPY(FB[:, 2 * g + kk, blk(2)], bk[:], [bb], [bFB[2 * g + kk][2]])
                else:
                    VCOPY(FB[:, 2 * g + kk, blk(2)], bk[:], [bb], [bFB[2 * g + kk][2]])

        mark("  oproj%d" % l)
        make_g2b2(0, l, 3, 2 * l)
        out_proj_and_ln(l, w_out_c[j], FB, bFB, 2 * l, 0)

    def out_block(b):
        flush_deferred()
        for i4 in range(4):
            t = 4 * b + i4
            oi = t % 2
            OTi = HB[:, 4 * oi:4 * oi + 4, 0:512].bitcast(F32)
            bo = [bHB[4 * oi + q][0] for q in range(4)]
            for half in range(2):
                bk, bb = next_bank()
                for q in range(4):
                    kc = half * 4 + q
                    TR(bk[:, q * 128:(q + 1) * 128], XS[:, kc, t * 128:(t + 1) * 128], IDENT[:], [bXS[kc][b], bIDENT], [bb])
                src = bk[:].rearrange("p (q n) -> p q n", q=2)
                if half == 0:
                    ACOPY(OTi[:, 0:2, :], src, [bb], bo)
                else:
                    VCOPY(OTi[:, 2:4, :], src, [bb], bo)
            DMA("sp", yout[t * 128:(t + 1) * 128, :].rearrange("r (q n) -> r q n", q=4), OTi, bo, [], is_out=True)

    def ffn(l):
        for f_ in range(FC):
            for b_ in range(NB):
                refence(bHID[f_][b_])
        w1v = ffn_w1[l].rearrange("(kc p) f -> p kc f", p=128)
        w2v = ffn_w2[l].rearrange("(fc p) d -> p fc d", p=128)
        agen = ada_gen(l + 1) if l + 1 < DEPTH else iter(())
        def w1_groups(slv, bs_, fq, bs):
            for b in bs:
                for fl in range(4):
                    fc = fq * 4 + fl
                    bk, bb = next_bank()
                    for kc in range(KC):
                        MM(bk[:], slv[:, kc, fl * 128:(fl + 1) * 128], HB[:, kc, blk(b)], kc == 0, kc == KC - 1,
                           [bs_, bHB[kc][b]], [bb])
                    hw = [bHID[fc][b]] + (bZSQ + bMR if fc < 4 else [])
                    ACT(HID[:, fc, blk(b)], bk[:], AF.Square, [bb], hw)
                    STT(HID[:, fc, blk(b)], bk[:], 0.0, HID[:, fc, blk(b)], ALU.is_gt, ALU.mult, [bb, bHID[fc][b]], [bHID[fc][b]])
                    bb.r.pop("act", None)

        def w1_load(fq):
            if fq < len(w1pre.get(l, ())):
                return w1pre[l][fq]
            sl, bs_ = next_slot()
            slv = sl.rearrange("p (k n) -> p k n", k=8)
            DMA("pool", slv, w1v[:, :, fq * 512:(fq + 1) * 512], [], [bs_])
            return slv, bs_

        s0 = w1_load(0)
        w1_groups(s0[0], s0[1], 0, (0, 1))
        next(agen, None)
        s1_ = w1_load(1)
        w1_groups(s1_[0], s1_[1], 1, (0, 1))
        w1_groups(s0[0], s0[1], 0, (2,))
        w1_groups(s1_[0], s1_[1], 1, (2,))
        next(agen, None)
        for fq in range(2, 8):
            slv, bs_ = w1_load(fq)
            if fq == 3:
                flush_deferred()
            w1_groups(slv, bs_, fq, range(NB))
            next(agen, None)
        mark("  w2_%d" % l)
        li2 = 2 * l + 1

        def w2_load(dc):
            sl, bs_ = next_slot()
            slv = sl.rearrange("p (f n) -> p f n", f=FC)
            DMA("pool", slv, w2v[:, :, dc * 128:(dc + 1) * 128], [], [bs_])
            return slv, bs_

        def w2_group(slv, bs_, dc, b):
            bk, bb = next_bank()
            for fc in range(FC):
                MM(bk[:], slv[:, fc, :], HID[:, fc, blk(b)], fc == 0, fc == FC - 1, [bs_, bHID[fc][b]], [bb])
            zstep(bk, bb, l, 5, dc, b)

        for dc in range(4):
            slv, bs_ = w2_load(dc)
            for b in range(NB):
                w2_group(slv, bs_, dc, b)
            next(agen, None)
        for _ in agen:
            pass
        if l + 1 < DEPTH:
            make_g2b2(1, l + 1, 0, 2 * l + 1)
        res = [w2_load(dc) for dc in range(4, 8)]
        mark("  ln2_%d" % l)
        cx = [ln_ctx_hb(1), ln_ctx_hb(2), ln_ctx_big(2, alias=ALIAS_LNT)]
        lastl = (l == DEPTH - 1)
        for b in range(NB):
            for i, dc in enumerate(range(4, 8)):
                w2_group(res[i][0], res[i][1], dc, b)
                if b > 0:
                    ln_norm(li2, b - 1, 1, cx[b - 1], kcs=(2 * i, 2 * i + 1))
            ln_sq(li2, b, cx[b])
            if lastl and b == NB - 1:
                out_block(0)
                out_block(1)
            ln_rstd(li2, b, cx[b])
        if l + 1 < DEPTH and (l + 1) % 2 == 0:
            even_pre(l + 1, (l + 1) // 2)
        ln_norm(li2, 2, 1, cx[2])
        if lastl:
            out_block(2)

    mark("ada0")
    agen0 = ada_gen(0)
    for _ in range(4):
        next(agen0)
    for kc in range(KC):
        for m, (t0, t1, bl) in enumerate(((0, 1024, (0, 1)), (1024, 1536, (2,)))):
            rd = [bXS[kc][b] for b in bl] + [bMOD[0]]
            wr = [bHB[kc][b] for b in bl]
            if alt():
                ACT(HB[:, kc, t0:t1], XS[:, kc, t0:t1], AF.Identity, rd, wr, scale=modsc(0, 1, kc, m), bias=modsc(0, 0, kc, m))
            else:
                TS(HB[:, kc, t0:t1], XS[:, kc, t0:t1], modsc(0, 1, kc, m), modsc(0, 0, kc, m), ALU.mult, ALU.add, rd, wr)

    for l in range(DEPTH):
        mark("mixer%d" % l)
        if l % 2 == 0:
            even_mixer(l, l // 2)
        else:
            odd_mixer(l, l // 2)
        mark("ffn%d" % l)
        ffn(l)
    mark("out")

    flush_deferred()
    bk, bb = next_bank()
    TR(bk[0:96, 0:128], ST[:, :], IDENT[:], [bST, bIDENT], [bb])
    STO = SMALL[0:96, 0:128] if False else MA[0:96, 0:128]
    VCOPY(STO, bk[0:96, 0:128], [bb], [bMA])
    DMA("sp", sout[:, :], STO, [bMA], [], is_out=True)

    P.finish()
    stats = P.emit(nc, es)
    stats["marks"] = marks
    es.close()
    return nc, stats


_CACHE = {}


def _dft_tables():
    if "dft" in _CACHE:
        return _CACHE["dft"]
    c = np.arange(256)
    ang = 2.0 * np.pi * np.outer(c, c) / 256.0
    cs = np.concatenate([np.cos(ang), np.sin(ang)], axis=1).astype(np.float32)
    s = np.arange(1024)
    a1 = 2.0 * np.pi * np.outer(s, s) / 1024.0
    sc1 = 1.0 / (16.0 * 32.0)
    pd1_s = np.stack([np.cos(a1) * sc1, -np.sin(a1) * sc1]).astype(np.float32)
    a2 = 2.0 * np.pi * np.outer(c, c) / 256.0
    sc2 = 1.0 / (16.0 * 16.0)
    blkc = (np.cos(a2) * sc2).astype(np.float32)
    blks = (-np.sin(a2) * sc2).astype(np.float32)
    pd1_p = np.zeros((2, 1024, 1024), np.float32)
    for i in range(4):
        pd1_p[0, i * 256:(i + 1) * 256, i * 256:(i + 1) * 256] = blkc
        pd1_p[1, i * 256:(i + 1) * 256, i * 256:(i + 1) * 256] = blks
    pd2 = pd1_p[:, :512, :512].copy()
    bf = ml_dtypes.bfloat16
    out = (cs.astype(bf), pd1_s.astype(bf), pd1_p.astype(bf), pd2.astype(bf))
    _CACHE["dft"] = out
    return out


def _prompt_ids(core):
    if core < 4:
        return None, [2 * core, 2 * core + 1]
    base = 8 + 6 * (core - 4)
    return [base + i for i in range(4)], [base + 4, base + 5]


def kernel(x_prompt, x_sample, state_lru, c, c_ctx, w_ada, b_ada, w_in_ab, conv_w, conv_b,
           lru_wa, lru_ba, lru_wx, lru_bx, lru_lam, sgu_ln_g, sgu_ln_b, sgu_ws, sgu_bs,
           w_out_ab, w_out_c, ffn_w1, ffn_w2, ln_g, ln_b):
    f32 = np.float32
    x_prompt = np.asarray(x_prompt, f32)
    x_sample = np.asarray(x_sample, f32)
    state_lru = np.asarray(state_lru, f32)
    c = np.asarray(c, f32)
    c_ctx = np.asarray(c_ctx, f32)
    cs, pd1_s, pd1_p, pd2 = _dft_tables()

    if "nc" not in _CACHE:
        _CACHE["nc"] = build_program()
    nc, stats = _CACHE["nc"]

    shared = {
        "w_ada": np.ascontiguousarray(w_ada, f32), "w_in_ab": np.ascontiguousarray(w_in_ab, f32),
        "lru_wa": np.ascontiguousarray(lru_wa, f32), "lru_wx": np.ascontiguousarray(lru_wx, f32),
        "sgu_ln_g": np.ascontiguousarray(sgu_ln_g, f32), "sgu_ln_b": np.ascontiguousarray(sgu_ln_b, f32),
        "sgu_ws": np.ascontiguousarray(sgu_ws, f32), "sgu_bs": np.ascontiguousarray(sgu_bs, f32),
        "w_out_ab": np.ascontiguousarray(w_out_ab, f32), "w_out_c": np.ascontiguousarray(w_out_c, f32),
        "ffn_w1": np.ascontiguousarray(ffn_w1, f32), "ffn_w2": np.ascontiguousarray(ffn_w2, f32),
        "pd2": pd2, "cs256": cs,
    }
    base = np.zeros((NROWS, 128), f32)
    base[R_BADA:R_BADA + 192] = np.asarray(b_ada, f32).reshape(192, 128)
    base[R_LNG:R_LNG + 64] = np.asarray(ln_g, f32).reshape(64, 128)
    base[R_LNB:R_LNB + 64] = np.asarray(ln_b, f32).reshape(64, 128)
    base[R_CONVW:R_CONVW + 32] = np.asarray(conv_w, f32).reshape(32, 128)
    base[R_CONVB:R_CONVB + 8] = np.asarray(conv_b, f32).reshape(8, 128)
    base[R_BA:R_BA + 16] = np.asarray(lru_ba, f32).reshape(16, 128)
    base[R_BX:R_BX + 16] = np.asarray(lru_bx, f32).reshape(16, 128)
    base[R_LAM:R_LAM + 16] = np.asarray(lru_lam, f32).reshape(16, 128)
    base[R_SGUG:R_SGUG + 8] = np.asarray(sgu_ln_g, f32).reshape(8, 128)

    in_maps = []
    for core in range(8):
        u1, u2 = _prompt_ids(core)
        pr = base.copy()
        if u1 is None:
            xs = [x_sample[core]] + [x_prompt[i] for i in u2]
            cond0 = c[core]
            pr[R_H0:R_H0 + 16] = state_lru[core].reshape(16, 128)
            cf = np.ones((128, 1), f32)
            pd1 = pd1_s
        else:
            xs = [x_prompt[i] for i in u1 + u2]
            cond0 = c_ctx
            cf = np.zeros((128, 1), f32)
            pd1 = pd1_p
        cond = np.stack([cond0.reshape(8, 128), c_ctx.reshape(8, 128)], axis=1)
        pr[R_COND:R_COND + 16] = cond.reshape(16, 128)
        m = dict(shared)
        m["xin"] = np.ascontiguousarray(np.concatenate(xs, axis=0), f32)
        m["params"] = pr
        m["cf"] = cf
        m["pd1"] = pd1
        in_maps.append(m)

    res = run_bass_kernel_spmd(nc, in_maps, core_ids=list(range(8)))
    rs = res.results

    y_prompt = np.zeros((32, 256, D), f32)
    y_sample = np.zeros((4, 1024, D), f32)
    new_state = np.zeros((32, 2, 2, 512), f32)
    for core in range(8):
        y = np.asarray(rs[core]["yout"], f32)
        s = np.asarray(rs[core]["sout"], f32).reshape(6, 2, 2, 512)
        u1, u2 = _prompt_ids(core)
        if u1 is None:
            y_sample[core] = y[0:1024]
        else:
            for i, pid in enumerate(u1):
                y_prompt[pid] = y[i * 256:(i + 1) * 256]
                new_state[pid] = s[i]
        for i, pid in enumerate(u2):
            y_prompt[pid] = y[1024 + i * 256:1024 + (i + 1) * 256]
            new_state[pid] = s[4 + i]
    return (y_prompt, y_sample, new_state)
```
